# Optimizing a Trainium2 kernel written in Bass

```python
import math
import jax
import jax.numpy as jnp
from jax import lax
import numpy as np

D_MODEL = 4096
BATCH = 4
SEQ = 4096
DEPTH = 2

HEAD_DIM = 128
N_HEADS = D_MODEL // HEAD_DIM
H_A = N_HEADS // 4
H_C = N_HEADS // 4
H_B = N_HEADS - H_A - H_C
KV_B = max(1, H_B // 8)
DIFF_DIM = HEAD_DIM // 2
MOBA_BLOCK = 256
MOBA_TOPK = 3
MOBA_QCHUNK = 16
WINDOW = 128
DIFF_QBLOCK = 128
N_BUCKETS = 32
MAX_DISTANCE = 128
D_FF = -(-8 * D_MODEL // (3 * 256)) * 256
ALPHA = (2.0 * DEPTH) ** 0.25
BETA = (8.0 * DEPTH) ** -0.25
LN_EPS = 1e-5
NEG = -1e30

QA_W = H_A * HEAD_DIM
QB_W = H_B * HEAD_DIM
KB_W = KV_B * HEAD_DIM
QC_W = H_C * 2 * DIFF_DIM
VC_W = H_C * HEAD_DIM
PROJ_SIZES = (QA_W, QA_W, QA_W, QB_W, KB_W, KB_W, QC_W, QC_W, VC_W)
V_SEGMENTS = (2, 5, 8)
SPLIT_POINTS = tuple(sum(PROJ_SIZES[:i + 1]) for i in range(len(PROJ_SIZES) - 1))
D_PROJ = sum(PROJ_SIZES)
D_CAT = QA_W + QB_W + VC_W

kernel_name = "hybrid_moba_swa_diff_adaln_deepnorm"


def rel_bucket(dist):
    n = jnp.maximum(dist, 0)
    max_exact = N_BUCKETS // 2
    nf = jnp.maximum(n, 1).astype(jnp.float32)
    large = max_exact + (jnp.log(nf / max_exact) / math.log(MAX_DISTANCE / max_exact)
                         * (N_BUCKETS - max_exact)).astype(jnp.int32)
    large = jnp.minimum(large, N_BUCKETS - 1)
    return jnp.where(n < max_exact, n, large)


def layer_norm(x, g, b):
    xf = x.astype(jnp.float32)
    mu = jnp.mean(xf, axis=-1, keepdims=True)
    var = jnp.mean(jnp.square(xf - mu), axis=-1, keepdims=True)
    return ((xf - mu) * lax.rsqrt(var + LN_EPS) * g + b).astype(x.dtype)


def rms_norm(x, g):
    xf = x.astype(jnp.float32)
    return (xf * lax.rsqrt(jnp.mean(jnp.square(xf), axis=-1, keepdims=True) + LN_EPS) * g).astype(x.dtype)


def moba_attention(q, k, v, tab):
    B, H, S, d = q.shape
    nb = -(-S // MOBA_BLOCK)
    pad = nb * MOBA_BLOCK - S
    kp = jnp.pad(k, ((0, 0), (0, 0), (0, pad), (0, 0))).reshape(B, H, nb, MOBA_BLOCK, d)
    vp = jnp.pad(v, ((0, 0), (0, 0), (0, pad), (0, 0))).reshape(B, H, nb, MOBA_BLOCK, d)
    k_mean = jnp.mean(kp.astype(jnp.float32), axis=3)
    qblk = jnp.arange(S) // MOBA_BLOCK
    gate = jnp.einsum('bhsd,bhnd->bhsn', q.astype(jnp.float32), k_mean)
    fully_past = jnp.arange(nb)[None, :] < qblk[:, None]
    gate = jnp.where(fully_past, gate, -jnp.inf)
    n_sel = min(MOBA_TOPK, nb)
    _, sel = lax.top_k(gate, n_sel)
    sel_ok = jnp.arange(n_sel)[None, :] < qblk[:, None]
    scale = d ** -0.5
    bi = jnp.arange(B)[:, None, None, None]
    hi = jnp.arange(H)[None, :, None, None]
    hi5 = jnp.arange(H)[None, :, None, None, None]
    offs = jnp.arange(MOBA_BLOCK)

    def chunk(ci):
        t0 = ci * MOBA_QCHUNK
        qc = lax.dynamic_slice_in_dim(q, t0, MOBA_QCHUNK, axis=2)
        sc = lax.dynamic_slice_in_dim(sel, t0, MOBA_QCHUNK, axis=2)
        ok = lax.dynamic_slice_in_dim(sel_ok, t0, MOBA_QCHUNK, axis=0)
        tq = t0 + jnp.arange(MOBA_QCHUNK)
        kg = kp[bi, hi, sc]
        vg = vp[bi, hi, sc]
        s_sel = jnp.einsum('bhqd,bhqnkd->bhqnk', qc, kg).astype(jnp.float32) * scale
        dist_sel = tq[None, None, :, None, None] - (sc[..., None] * MOBA_BLOCK + offs)
        s_sel = s_sel + tab[hi5, rel_bucket(dist_sel)]
        s_sel = jnp.where(ok[None, None, :, :, None], s_sel, NEG)
        ob = t0 // MOBA_BLOCK
        ko = lax.dynamic_slice_in_dim(kp, ob, 1, axis=2)[:, :, 0]
        vo = lax.dynamic_slice_in_dim(vp, ob, 1, axis=2)[:, :, 0]
        s_own = jnp.einsum('bhqd,bhkd->bhqk', qc, ko).astype(jnp.float32) * scale
        dist_own = tq[:, None] - (ob * MOBA_BLOCK + offs)[None, :]
        s_own = s_own + tab[:, rel_bucket(dist_own)][None]
        s_own = jnp.where((dist_own >= 0)[None, None], s_own, NEG)
        logits = jnp.concatenate([s_sel.reshape(B, H, MOBA_QCHUNK, n_sel * MOBA_BLOCK), s_own], axis=-1)
        p = jax.nn.softmax(logits, axis=-1).astype(v.dtype)
        p_sel = p[..., :n_sel * MOBA_BLOCK].reshape(B, H, MOBA_QCHUNK, n_sel, MOBA_BLOCK)
        p_own = p[..., n_sel * MOBA_BLOCK:]
        return (jnp.einsum('bhqnk,bhqnkd->bhqd', p_sel, vg)
                + jnp.einsum('bhqk,bhkd->bhqd', p_own, vo))

    outs = lax.map(chunk, jnp.arange(S // MOBA_QCHUNK))
    return outs.transpose(1, 2, 0, 3, 4).reshape(B, H, S, d)


def swa_sink_attention(q, k, v, sinks, tab):
    B, S, H, d = q.shape
    KV = k.shape[2]
    G = H // KV
    nq = S // WINDOW
    qb = q.reshape(B, nq, WINDOW, KV, G, d)
    kb = k.reshape(B, nq, WINDOW, KV, d)
    vb = v.reshape(B, nq, WINDOW, KV, d)
    kband = jnp.concatenate([jnp.pad(kb, ((0, 0), (1, 0), (0, 0), (0, 0), (0, 0)))[:, :-1], kb], axis=2)
    vband = jnp.concatenate([jnp.pad(vb, ((0, 0), (1, 0), (0, 0), (0, 0), (0, 0)))[:, :-1], vb], axis=2)
    s = jnp.einsum('bnqkgd,bnjkd->bkgnqj', qb, kband).astype(jnp.float32) * (d ** -0.5)
    i = jnp.arange(WINDOW)
    j = jnp.arange(2 * WINDOW)
    dist = i[:, None] + WINDOW - j[None, :]
    band = (dist >= 0) & (dist < WINDOW)
    valid = band[None] & ((jnp.arange(nq)[:, None, None] > 0) | (j[None, None, :] >= WINDOW))
    bias = tab[:, rel_bucket(dist)].reshape(KV, G, WINDOW, 2 * WINDOW)
    s = s + bias[None, :, :, None]
    s = jnp.where(valid[None, None, None], s, NEG)
    sink = jnp.broadcast_to(sinks.astype(jnp.float32).reshape(1, KV, G, 1, 1, 1), s.shape[:-1] + (1,))
    p = jax.nn.softmax(jnp.concatenate([s, sink], axis=-1), axis=-1)[..., :-1].astype(v.dtype)
    out = jnp.einsum('bkgnqj,bnjkd->bnqkgd', p, vband)
    return out.reshape(B, S, H, d)


def diff_attention(q, k, v, lam, subln_g, lam_init, tab):
    B, H, _, S, dd = q.shape
    scale = dd ** -0.5
    kpos = jnp.arange(S)

    def block(bi):
        t0 = bi * DIFF_QBLOCK
        qc = lax.dynamic_slice_in_dim(q, t0, DIFF_QBLOCK, axis=3)
        s = jnp.einsum('bhmqd,bhmkd->bhmqk', qc, k).astype(jnp.float32) * scale
        dist = (t0 + jnp.arange(DIFF_QBLOCK))[:, None] - kpos[None, :]
        s = s + tab[:, rel_bucket(dist)][None, :, None]
        s = jnp.where((dist >= 0)[None, None, None], s, NEG)
        p = jax.nn.softmax(s, axis=-1)
        a = (p[:, :, 0] - lam * p[:, :, 1]).astype(v.dtype)
        return jnp.einsum('bhqk,bhkd->bhqd', a, v)

    o = lax.map(block, jnp.arange(S // DIFF_QBLOCK))
    o = o.transpose(1, 2, 0, 3, 4).reshape(B, H, S, 2 * dd)
    return rms_norm(o, subln_g) * (1.0 - lam_init)


def hybrid_layer(x, c, tab_a, tab_b, tab_c, w_ada, b_ada, w_in, w_o, sinks, lam_p, subln_g,
                 ln_g, ln_b, w_gate, w_up, w_down, layer_idx):
    B, S, D = x.shape
    mod = jax.nn.silu(c) @ w_ada + b_ada
    sh1, sc1, g1, sh2, sc2, g2 = jnp.split(mod, 6, axis=-1)
    h = x * (1.0 + sc1[:, None]) + sh1[:, None]
    proj = h @ w_in
    qa, ka, va, qb, kb, vb, qc, kc, vc = jnp.split(proj, SPLIT_POINTS, axis=-1)
    to_heads = lambda t, n: t.reshape(B, S, n, HEAD_DIM).transpose(0, 2, 1, 3)
    o_a = moba_attention(to_heads(qa, H_A), to_heads(ka, H_A), to_heads(va, H_A), tab_a)
    o_a = o_a.transpose(0, 2, 1, 3).reshape(B, S, QA_W)
    o_b = swa_sink_attention(qb.reshape(B, S, H_B, HEAD_DIM), kb.reshape(B, S, KV_B, HEAD_DIM),
                             vb.reshape(B, S, KV_B, HEAD_DIM), sinks, tab_b).reshape(B, S, QB_W)
    lam_init = 0.8 - 0.6 * math.exp(-0.3 * layer_idx)
    lp = lam_p.astype(jnp.float32)
    lam = jnp.exp(jnp.sum(lp[0] * lp[1])) - jnp.exp(jnp.sum(lp[2] * lp[3])) + lam_init
    to_diff = lambda t: t.reshape(B, S, H_C, 2, DIFF_DIM).transpose(0, 2, 3, 1, 4)
    o_c = diff_attention(to_diff(qc), to_diff(kc), to_heads(vc, H_C), lam, subln_g, lam_init, tab_c)
    o_c = o_c.transpose(0, 2, 1, 3).reshape(B, S, VC_W)
    mix = jnp.concatenate([o_a, o_b, o_c], axis=-1) @ w_o
    x = layer_norm(ALPHA * x + g1[:, None] * mix, ln_g[0], ln_b[0])
    h = x * (1.0 + sc2[:, None]) + sh2[:, None]
    ff = (jax.nn.silu(h @ w_gate) * (h @ w_up)) @ w_down
    return layer_norm(ALPHA * x + g2[:, None] * ff, ln_g[1], ln_b[1])


def setup_inputs(seed: int = 0) -> dict:
    key = jax.random.key(seed)
    ks = jax.random.split(key, 16)
    f32 = jnp.float32
    nrm = lambda k, shape: jax.random.normal(k, shape, f32)
    col_scale = jnp.concatenate([jnp.full((n,), BETA if i in V_SEGMENTS else 1.0, f32)
                                 for i, n in enumerate(PROJ_SIZES)])
    return {
        "x": nrm(ks[0], (BATCH, SEQ, D_MODEL)),
        "c": nrm(ks[1], (BATCH, D_MODEL)),
        "rel_bias": 0.3 * nrm(ks[2], (N_BUCKETS, N_HEADS)),
        "w_ada": nrm(ks[3], (DEPTH, D_MODEL, 6 * D_MODEL)) * (0.5 * D_MODEL ** -0.5),
        "b_ada": 0.01 * nrm(ks[4], (DEPTH, 6 * D_MODEL)),
        "w_in": nrm(ks[5], (DEPTH, D_MODEL, D_PROJ)) * (D_MODEL ** -0.5) * col_scale,
        "w_o": nrm(ks[6], (DEPTH, D_CAT, D_MODEL)) * (BETA * D_CAT ** -0.5),
        "attn_sinks": nrm(ks[7], (DEPTH, H_B)),
        "diff_lambda": 0.1 * nrm(ks[8], (DEPTH, 4, DIFF_DIM)),
        "diff_subln_g": 1.0 + 0.02 * nrm(ks[9], (DEPTH, HEAD_DIM)),
        "ln_g": 1.0 + 0.02 * nrm(ks[10], (DEPTH, 2, D_MODEL)),
        "ln_b": 0.02 * nrm(ks[11], (DEPTH, 2, D_MODEL)),
        "w_gate": nrm(ks[12], (DEPTH, D_MODEL, D_FF)) * (D_MODEL ** -0.5),
        "w_up": nrm(ks[13], (DEPTH, D_MODEL, D_FF)) * (D_MODEL ** -0.5),
        "w_down": nrm(ks[14], (DEPTH, D_FF, D_MODEL)) * (BETA * D_FF ** -0.5),
    }


def reference(x, c, rel_bias, w_ada, b_ada, w_in, w_o, attn_sinks, diff_lambda, diff_subln_g,
              ln_g, ln_b, w_gate, w_up, w_down):
    tab_a = rel_bias[:, :H_A].T
    tab_b = rel_bias[:, H_A:H_A + H_B].T
    tab_c = rel_bias[:, H_A + H_B:].T
    for l in range(DEPTH):
        x = hybrid_layer(x, c, tab_a, tab_b, tab_c, w_ada[l], b_ada[l], w_in[l], w_o[l],
                         attn_sinks[l], diff_lambda[l], diff_subln_g[l], ln_g[l], ln_b[l],
                         w_gate[l], w_up[l], w_down[l], l)
    return x
```

```python
import math
from contextlib import ExitStack

import numpy as np
import concourse.bass as bass
import concourse.mybir as mybir
from concourse.bass_utils import run_bass_kernel_spmd

F32 = mybir.dt.float32
BF16 = mybir.dt.bfloat16
AF = mybir.ActivationFunctionType
ALU = mybir.AluOpType
AX = mybir.AxisListType

D = 4096
KC = D // 128
DEPTH = 2
HD = 128
H_A, H_B, H_C, KV_B = 8, 16, 8, 2
D_PROJ = 8704
D_FF = 11008
FC = D_FF // 128
FF_HALVES = ((0, 30), (30, 58), (58, 86))
N_BUCKETS = 32
MAX_DISTANCE = 128
ALPHA = (2.0 * DEPTH) ** 0.25
LN_EPS = 1e-5
NEG = -1e30
MNEG = 30000.0
NCORES = 8
CH_QA, CH_KA, CH_VA, CH_QB, CH_KB, CH_VB, CH_QC, CH_KC, CH_VC = 0, 8, 16, 24, 40, 42, 44, 52, 60
V_COLS = 2304
VOFF_A, VOFF_B, VOFF_C = 0, 1024, 1280
TT = 512


class Buf:
    __slots__ = ("name", "last_w", "readers")

    def __init__(self, name):
        self.name = name
        self.last_w = {}
        self.readers = {}


class Eng:
    def __init__(self, name, lazy):
        self.name = name
        self.count = 0
        self.pending = False
        self.waited = {}
        self.ops = []
        self.lazy = lazy


class Prog:
    ENGS = ("tensor", "vector", "scalar", "gpsimd", "sync")

    def __init__(self, nc, stack):
        self.nc = nc
        self.eng = {n: Eng(n, lazy=(n == "tensor")) for n in self.ENGS}
        self.sems = {}
        self._stack = stack
        self.latest = {}
        self.semcount = {}
        self.n_ops = 0
        for n in self.ENGS:
            self.sems["E_" + n] = stack.enter_context(nc.semaphore("E_" + n))

    def _sem_for(self, buf, prefix):
        key = prefix + buf.name
        if key not in self.sems:
            self.sems[key] = self._stack.enter_context(self.nc.semaphore(key))
            self.semcount[key] = 0
        return key

    def _need(self, e, toks, skip=None):
        best = {}
        for k, v in toks:
            if k == skip:
                continue
            if e.waited.get(k, 0) >= v:
                continue
            if best.get(k, 0) < v:
                best[k] = v
        for k, v in best.items():
            e.waited[k] = v
            e.ops.append(("wait", self.sems[k], v))

    @staticmethod
    def _deps(reads, writes):
        toks = []
        for b in reads:
            toks.extend(b.last_w.items())
        for b in writes:
            toks.extend(b.last_w.items())
            toks.extend(b.readers.items())
        return toks

    def _commit(self, tok, reads, writes):
        k, v = tok
        for b in reads:
            if b.readers.get(k, 0) < v:
                b.readers[k] = v
        for b in writes:
            b.last_w[k] = v
            b.readers = {}
        if self.latest.get(k, 0) < v:
            self.latest[k] = v
        self.n_ops += 1

    def op(self, engine, fn, reads=(), writes=(), sig=True):
        e = self.eng[engine]
        ekey = "E_" + engine
        self._need(e, self._deps(reads, writes), skip=(ekey if engine == "tensor" else None))
        if sig:
            e.count += 1
            tok = (ekey, e.count)
            e.ops.append(("op", fn, self.sems[ekey]))
        else:
            tok = (ekey, e.count + 1)
            e.ops.append(("op", fn, None))
        e.pending = not sig
        self._commit(tok, reads, writes)
        return tok

    def dma(self, queue, fn, reads=(), writes=(), sembuf=None):
        e = self.eng[queue]
        key = self._sem_for(sembuf, "D_")
        self._need(e, self._deps(reads, writes))
        self.semcount[key] += 16
        tok = (key, self.semcount[key])
        e.ops.append(("dma", fn, self.sems[key]))
        self._commit(tok, reads, writes)
        return tok

    def cc(self, fn, reads=(), writes=(), sembuf=None):
        e = self.eng["gpsimd"]
        key = self._sem_for(sembuf, "C_")
        self._need(e, self._deps(reads, writes))
        self.semcount[key] += 1
        tok = (key, self.semcount[key])
        e.ops.append(("cc", fn, self.sems[key]))
        self._commit(tok, reads, writes)
        return tok

    def barrier(self):
        pe = self.eng["tensor"]
        if pe.pending:
            raise RuntimeError("barrier with unsignaled PE op pending")
        toks = list(self.latest.items())
        for n in self.ENGS:
            self._need(self.eng[n], toks)

    def emit(self, block):
        for name in self.ENGS:
            e = self.eng[name]
            if not e.ops:
                continue
            if e.lazy and e.pending:
                raise RuntimeError("last op on lazy engine must be signaled")

            def body(h, e=e):
                for o in e.ops:
                    if o[0] == "wait":
                        h.wait_ge(o[1], o[2])
                    elif o[0] == "op":
                        ins = o[1](h)
                        if o[2] is not None:
                            ins.then_inc(o[2], 1)
                    elif o[0] == "cc":
                        o[1](h).then_inc(o[2])
                    else:
                        o[1](h).then_inc(o[2], 16)
            getattr(block, name)(body)


class Ring:
    def __init__(self, tiles):
        self.tiles = tiles
        self.i = 0

    def next(self):
        t = self.tiles[self.i % len(self.tiles)]
        self.i += 1
        return t


def rel_bucket_np(dist):
    n = np.maximum(dist, 0)
    max_exact = N_BUCKETS // 2
    nf = np.maximum(n, 1).astype(np.float32)
    large = max_exact + (np.log(nf / np.float32(max_exact)) / np.float32(math.log(MAX_DISTANCE / max_exact))
                         * np.float32(N_BUCKETS - max_exact)).astype(np.int32)
    large = np.minimum(large, N_BUCKETS - 1)
    return np.where(n < max_exact, n, large)


def build_program(S, depth=DEPTH, phases="swmABC", dbg=False):
    NT = S // TT
    NKT = S // 128
    NBLK = S // 256
    nc = bass.Bass("TRN2", target_bir_lowering=False)
    st = ExitStack()

    def din(name, shape, dt=F32):
        return nc.dram_tensor(name, shape, dt, kind="ExternalInput").ap()

    xT = din("xT", [KC, 128, S])
    cTs = din("cTs", [128, 4, 4])
    bsel = din("bsel", [4, 2])
    w_ada_s = din("w_ada_s", [depth, 512, 6 * D]) if "m" in phases else None
    b_ada_t = din("b_ada_t", [depth, 128, 192])
    wsh = {
        "in": din("w_in_s", [depth, D // NCORES, D_PROJ]),
        "o": din("w_o_s", [depth, D // NCORES, D]),
        "gate": din("w_gate_s", [depth, D // NCORES, D_FF]),
        "up": din("w_up_s", [depth, D // NCORES, D_FF]),
        "down": din("w_down_s", [depth, D_FF // NCORES, D]),
    } if "w" in phases else None
    wshape = {"in": (D, D_PROJ), "o": (D, D), "gate": (D, D_FF), "up": (D, D_FF), "down": (D_FF, D)}
    lng_t = din("lng_t", [depth, 2, 128, KC])
    lnb_t = din("lnb_t", [depth, 2, 128, KC])
    sinks_rep = din("sinks_rep", [depth, 128, H_B])
    lam_rep = din("lam_rep", [depth, 128, 256])
    subg_t = din("subg_t", [128, depth])
    bdiag = din("bdiag", [32, 128, 128])
    bprev = din("bprev", [32, 128, 128])
    c31_rep = din("c31_rep", [128, 32])
    pastmask = din("pastmask", [128, NT * 4, 16])
    e_all = din("e_all", [16, 16 * 128], BF16)
    ident_in = din("ident", [128, 128])
    outT = nc.dram_tensor("outT", [KC, 128, S], F32, kind="ExternalOutput").ap()

    wsrc, wfull = {}, {}
    for l in range(depth):
        for k, (r, c) in wshape.items():
            wsrc[(k, l)] = nc.dram_tensor("wsrc_%s%d" % (k, l), [r // NCORES, c], BF16).ap()
            wfull[(k, l)] = nc.dram_tensor("wfull_%s%d" % (k, l), [r, c], BF16).ap()
    modp = nc.dram_tensor("modp", [4, depth * 6 * D], F32).ap()
    modr = nc.dram_tensor("modr", [4, depth * 6 * D], F32).ap()
    QKT = nc.dram_tensor("QKT", [68, 128, S], BF16).ap()
    VTOK = nc.dram_tensor("VTOK", [S, V_COLS], BF16).ap()
    OT = nc.dram_tensor("OT", [KC, 128, S], BF16).ap()
    X1T = nc.dram_tensor("X1T", [KC, 128, S], F32).ap()

    P = Prog(nc, st)
    RG = [list(range(NCORES))]

    B_wsrc = {k: Buf("wsrc_%s%d" % k) for k in wsrc}
    B_wfull = {k: Buf("wfull_%s%d" % k) for k in wfull}
    B_modp, B_modr = Buf("modp"), Buf("modr")
    B_QKT, B_VTOK, B_OT, B_X1T, B_out = Buf("QKT"), Buf("VTOK"), Buf("OT"), Buf("X1T"), Buf("outT")

    uid = [0]

    def sb(stack, name, shape, dt):
        uid[0] += 1
        return stack.enter_context(nc.sbuf_tensor("%s_u%d" % (name, uid[0]), shape, dt)), Buf(name)

    def sbc(stack, name, shape, dt, n):
        uid[0] += 1
        return stack.enter_context(nc.sbuf_tensor("%s_u%d" % (name, uid[0]), shape, dt)), [Buf("%s_c%d" % (name, i)) for i in range(n)]

    ps = []
    for i in range(8):
        ps.append((st.enter_context(nc.psum_tensor("ps%d" % i, [128, 512], F32)), Buf("ps%d" % i)))

    modv, B_modv = sb(st, "modv", [128, depth, 192], F32)
    lng, B_lng = sb(st, "lng", [128, depth * 2, KC], F32)
    lnb, B_lnb = sb(st, "lnb", [128, depth * 2, KC], F32)
    ones_f, B_ones_f = sb(st, "ones_f", [128, 128], F32)
    ones_b, B_ones_b = sb(st, "ones_b", [128, 128], BF16)
    ident, B_ident = sb(st, "ident", [128, 128], F32)
    c31, B_c31 = sb(st, "c31", [128, 32], F32)
    eall, B_eall = sb(st, "eall", [16, 16 * 128], BF16)
    sexp, B_sexp = sb(st, "sexp", [128, depth, H_B], F32)
    lamv, B_lamv = sb(st, "lamv", [128, depth, 4], F32)
    subg, B_subg = sb(st, "subg", [128, depth], F32)

    block = st.enter_context(nc.Block())

    rr = {"cast": 0, "evac": 0}

    def cast_op(out, in_, reads, writes, engines=("vector", "scalar")):
        e = engines[rr["cast"] % len(engines)]
        rr["cast"] += 1
        if e == "scalar":
            return P.op("scalar", lambda h: h.activation(out=out, in_=in_, func=AF.Copy), reads, writes)
        return P.op(e, lambda h: h.tensor_copy(out=out, in_=in_), reads, writes)

    def evac_op(out, in_, reads, writes):
        e = ("vector", "scalar")[rr["evac"] % 2]
        rr["evac"] += 1
        if e == "scalar":
            return P.op("scalar", lambda h: h.activation(out=out, in_=in_, func=AF.Copy), reads, writes)
        return P.op("vector", lambda h: h.tensor_copy(out=out, in_=in_), reads, writes)

    P.dma("sync", lambda h: h.dma_start(out=ident[:], in_=ident_in[:]), writes=[B_ident], sembuf=B_ident)
    P.dma("sync", lambda h: h.dma_start(out=c31[:], in_=c31_rep[:]), writes=[B_c31], sembuf=B_c31)
    P.dma("sync", lambda h: h.dma_start(out=eall[:], in_=e_all[:]), writes=[B_eall], sembuf=B_eall)
    P.dma("sync", lambda h: h.dma_start(out=subg[:], in_=subg_t[:]), writes=[B_subg], sembuf=B_subg)
    P.dma("sync", lambda h: h.dma_start(out=lng[:], in_=lng_t.rearrange("l t p c -> p (l t) c")), writes=[B_lng], sembuf=B_lng)
    P.dma("sync", lambda h: h.dma_start(out=lnb[:], in_=lnb_t.rearrange("l t p c -> p (l t) c")), writes=[B_lnb], sembuf=B_lnb)
    P.op("vector", lambda h: h.memset(ones_f[:], 1.0), writes=[B_ones_f])
    P.op("vector", lambda h: h.memset(ones_b[:], 1.0), writes=[B_ones_b])

    def phase_weights(l):
        with ExitStack() as ph:
            PIECE = 4096
            fr = Ring([sb(ph, "wc_f%d" % i, [128, PIECE], F32) for i in range(3)])
            br = Ring([sb(ph, "wc_b%d" % i, [128, PIECE], BF16) for i in range(3)])
            for k in ("in", "o", "gate", "up", "down"):
                r, c = wshape[k]
                per = (r // NCORES) * c // 128
                src_flat = wsh[k][l].rearrange("r c -> (r c)").rearrange("(p f) -> p f", p=128)
                dst_flat = wsrc[(k, l)].rearrange("r c -> (r c)").rearrange("(p f) -> p f", p=128)
                f0 = 0
                while f0 < per:
                    n = min(PIECE, per - f0)
                    (ft, fb), (bt, bb) = fr.next(), br.next()
                    P.dma("sync", lambda h, ft=ft, f0=f0, n=n, src_flat=src_flat: h.dma_start(out=ft[:, 0:n], in_=src_flat[:, f0:f0 + n]),
                          writes=[fb], sembuf=fb)
                    cast_op(bt[:, 0:n], ft[:, 0:n], [fb], [bb])
                    P.dma("sync", lambda h, bt=bt, f0=f0, n=n, dst_flat=dst_flat: h.dma_start(out=dst_flat[:, f0:f0 + n], in_=bt[:, 0:n]),
                          reads=[bb], writes=[B_wsrc[(k, l)]], sembuf=bb)
                    f0 += n
                P.cc(lambda h, k=k: h.collective_compute("AllGather", ALU.bypass, replica_groups=RG,
                                                         ins=[wsrc[(k, l)][:]], outs=[wfull[(k, l)][:]]),
                     reads=[B_wsrc[(k, l)]], writes=[B_wfull[(k, l)]], sembuf=B_wfull[(k, l)])
            P.barrier()

    def phase_mod():
        with ExitStack() as ph:
            cs_in, B_cs_in = sb(ph, "cs_in", [128, 4, 4], F32)
            cs, B_cs = sb(ph, "cs", [128, 4, 4], F32)
            bs, B_bs = sb(ph, "bs", [4, 2], F32)
            wr = Ring([sb(ph, "wada%d" % i, [128, 4, 2048], F32) for i in range(2)])
            sr = Ring([sb(ph, "mst%d" % i, [4, 2048], F32) for i in range(2)])
            m4r = Ring([sb(ph, "m4_%d" % i, [4, D], F32) for i in range(2)])
            bad, B_bad = sb(ph, "bad", [128, depth, 192], F32)
            P.dma("sync", lambda h: h.dma_start(out=cs_in[:], in_=cTs[:]), writes=[B_cs_in], sembuf=B_cs_in)
            P.dma("sync", lambda h: h.dma_start(out=bs[:], in_=bsel[:]), writes=[B_bs], sembuf=B_bs)
            P.dma("sync", lambda h: h.dma_start(out=bad[:], in_=b_ada_t.rearrange("l p c -> p l c")), writes=[B_bad], sembuf=B_bad)
            P.op("scalar", lambda h: h.activation(out=cs[:], in_=cs_in[:], func=AF.Silu), [B_cs_in], [B_cs])
            pi = 0
            for l in range(depth):
                wv = w_ada_s[l].rearrange("(c p) n -> p c n", p=128)
                for pc in range(6 * D // 2048):
                    wt, wb = wr.next()
                    P.dma("sync", lambda h, wt=wt, pc=pc, wv=wv: h.dma_start(out=wt[:], in_=wv[:, :, pc * 2048:(pc + 1) * 2048]),
                          writes=[wb], sembuf=wb)
                    stt, stb = sr.next()
                    for nb in range(4):
                        pt, pb = ps[pi % 8]
                        pi += 1
                        for kc in range(4):
                            P.op("tensor", lambda h, pt=pt, wt=wt, kc=kc, nb=nb: h.matmul(
                                pt[0:4, :], lhsT=cs[:, kc, :], rhs=wt[:, kc, nb * 512:(nb + 1) * 512], start=(kc == 0), stop=(kc == 3)),
                                reads=[B_cs, wb], writes=[pb], sig=(kc == 3))
                        evac_op(stt[:, nb * 512:(nb + 1) * 512], pt[0:4, :], [pb], [stb])
                    col = l * 6 * D + pc * 2048
                    P.dma("sync", lambda h, stt=stt, col=col: h.dma_start(out=modp[:, col:col + 2048], in_=stt[:]),
                          reads=[stb], writes=[B_modp], sembuf=stb)
            mp2 = modp.rearrange("b n -> (b n)").rearrange("(p f) -> p f", p=128)
            mr2 = modr.rearrange("b n -> (b n)").rearrange("(p f) -> p f", p=128)
            P.cc(lambda h: h.collective_compute("AllReduce", ALU.add, replica_groups=RG, ins=[mp2], outs=[mr2]),
                 reads=[B_modp], writes=[B_modr], sembuf=B_modr)
            for l in range(depth):
                for v in range(6):
                    mt, mb = m4r.next()
                    col = (l * 6 + v) * D
                    P.dma("sync", lambda h, mt=mt, col=col: h.dma_start(out=mt[:], in_=modr[:, col:col + D]),
                          reads=[B_modr], writes=[mb], sembuf=mb)
                    pt, pb = ps[pi % 8]
                    pi += 1
                    for c in range(KC):
                        P.op("tensor", lambda h, pt=pt, mt=mt, c=c: h.matmul(
                            pt[:, 2 * c:2 * c + 2], lhsT=mt[:, c * 128:(c + 1) * 128], rhs=bs[:, :], start=True, stop=True),
                            reads=[mb, B_bs], writes=[pb], sig=(c == KC - 1))
                    pv = pt[:, 0:2 * KC].rearrange("p (c two) -> p c two", two=2)[:, :, 0]
                    if v in (1, 4):
                        P.op("vector", lambda h, pv=pv, l=l, v=v: h.scalar_tensor_tensor(
                            out=modv[:, l, v * 32:(v + 1) * 32], in0=pv, scalar=1.0, in1=bad[:, l, v * 32:(v + 1) * 32],
                            op0=ALU.add, op1=ALU.add), [pb, B_bad], [B_modv])
                    else:
                        P.op("vector", lambda h, pv=pv, l=l, v=v: h.tensor_tensor(
                            out=modv[:, l, v * 32:(v + 1) * 32], in0=pv, in1=bad[:, l, v * 32:(v + 1) * 32], op=ALU.add),
                            [pb, B_bad], [B_modv])
            P.barrier()

    class WStream:
        def __init__(self, ring):
            self.ring = ring
            self.jobs = []
            self.issued = 0
            self.slots = []

        def add(self, parts):
            self.jobs.append(parts)
            return len(self.jobs) - 1

        def _issue(self, j):
            t, b = self.ring.next()
            for (key, k0, kn, c0, cn, dc) in self.jobs[j]:
                wv = wfull[key].rearrange("(c p) n -> p c n", p=128)
                P.dma("sync", lambda h, t=t, wv=wv, k0=k0, kn=kn, c0=c0, cn=cn, dc=dc: h.dma_start(
                    out=t[:, 0:kn, dc:dc + cn], in_=wv[:, k0:k0 + kn, c0:c0 + cn]),
                    reads=[B_wfull[key]], writes=[b], sembuf=b)
            self.slots.append((t, b))

        def get(self, j):
            la = len(self.ring.tiles) - 1
            while self.issued < min(len(self.jobs), j + la + 1):
                self._issue(self.issued)
                self.issued += 1
            return self.slots[j]

    def phase_A(l, xsrc, B_xsrc):
        with ExitStack() as ph:
            hT, B_hT = sbc(ph, "hT", [128, KC, TT], BF16, KC)
            xr = Ring([sb(ph, "xs%d" % i, [128, 4, TT], F32) for i in range(3)])
            wring = Ring([sb(ph, "wA%d" % i, [128, 8, 512], BF16) for i in range(4)])
            ostr = Ring([sb(ph, "ostA%d" % i, [128, 512], BF16) for i in range(4)])
            sc1p = modv[:, l, 32:64]
            sh1 = modv[:, l, 0:32]
            fm_blocks = []
            for lo, hi in ((0, 16), (24, 42), (44, 60)):
                c = lo
                while c < hi:
                    n = min(4, hi - c)
                    fm_blocks.append((c, n))
                    c += n
            tm_blocks = [(CH_VA * 128, 512, VOFF_A), (CH_VA * 128 + 512, 512, VOFF_A + 512), (CH_VB * 128, 256, VOFF_B),
                         (CH_VC * 128, 512, VOFF_C), (CH_VC * 128 + 512, 512, VOFF_C + 512)]
            for t in range(NT):
                t0 = t * TT
                for g in range(KC // 4):
                    xt, xb = xr.next()
                    P.dma("sync", lambda h, xt=xt, g=g, t0=t0: h.dma_start(
                        out=xt[:], in_=xsrc[g * 4:(g + 1) * 4, :, t0:t0 + TT].rearrange("c p s -> p c s")),
                        reads=[B_xsrc], writes=[xb], sembuf=xb)
                    for j in range(4):
                        kc = g * 4 + j
                        if kc % 2 == 0:
                            P.op("scalar", lambda h, xt=xt, j=j, kc=kc: h.activation(
                                out=hT[:, kc, :], in_=xt[:, j, :], func=AF.Identity, scale=sc1p[:, kc:kc + 1], bias=sh1[:, kc:kc + 1]),
                                [xb, B_modv], [B_hT[kc]])
                        else:
                            P.op("vector", lambda h, xt=xt, j=j, kc=kc: h.tensor_scalar(
                                out=hT[:, kc, :], in0=xt[:, j, :], scalar1=sc1p[:, kc:kc + 1], scalar2=sh1[:, kc:kc + 1],
                                op0=ALU.mult, op1=ALU.add), [xb, B_modv], [B_hT[kc]])
                ws = WStream(wring)
                plan = []
                for (c, n) in fm_blocks:
                    plan.append(("fm", c, n, [ws.add([(("in", l), kg * 8, 8, c * 128, n * 128, 0)]) for kg in range(4)]))
                for (c0, cn, vo) in tm_blocks:
                    plan.append(("tm", c0, cn, vo, [ws.add([(("in", l), kg * 8, 8, c0, cn, 0)]) for kg in range(4)]))
                bank = 0
                for item in plan:
                    if item[0] == "fm":
                        _, c, n, jobs = item
                        banks = [ps[(bank + i) % 8] for i in range(n)]
                        bank += 4
                        for kg, j in enumerate(jobs):
                            wt, wb = ws.get(j)
                            for k in range(8):
                                kk = kg * 8 + k
                                for i in range(n):
                                    pt, pb = banks[i]
                                    P.op("tensor", lambda h, pt=pt, wt=wt, k=k, i=i, kk=kk: h.matmul(
                                        pt[:], lhsT=wt[:, k, i * 128:(i + 1) * 128], rhs=hT[:, kk, :], start=(kk == 0), stop=(kk == KC - 1)),
                                        reads=[wb, B_hT[kk]], writes=[pb], sig=(kk == KC - 1) or (k == 7 and i == n - 1))
                        for i in range(n):
                            pt, pb = banks[i]
                            ot, ob = ostr.next()
                            evac_op(ot[:], pt[:], [pb], [ob])
                            P.dma("gpsimd", lambda h, ot=ot, ch=c + i, t0=t0: h.dma_start(out=QKT[ch, :, t0:t0 + TT], in_=ot[:]),
                                  reads=[ob], writes=[B_QKT], sembuf=ob)
                    else:
                        _, c0, cn, vo, jobs = item
                        banks = [ps[(bank + i) % 8] for i in range(4)]
                        bank += 4
                        for kg, j in enumerate(jobs):
                            wt, wb = ws.get(j)
                            for k in range(8):
                                kk = kg * 8 + k
                                for ts in range(4):
                                    pt, pb = banks[ts]
                                    P.op("tensor", lambda h, pt=pt, wt=wt, k=k, ts=ts, kk=kk, cn=cn: h.matmul(
                                        pt[:, 0:cn], lhsT=hT[:, kk, ts * 128:(ts + 1) * 128], rhs=wt[:, k, 0:cn],
                                        start=(kk == 0), stop=(kk == KC - 1)),
                                        reads=[wb, B_hT[kk]], writes=[pb], sig=(kk == KC - 1) or (k == 7 and ts == 3))
                        for ts in range(4):
                            pt, pb = banks[ts]
                            ot, ob = ostr.next()
                            evac_op(ot[:, 0:cn], pt[:, 0:cn], [pb], [ob])
                            P.dma("gpsimd", lambda h, ot=ot, r0=t0 + ts * 128, vo=vo, cn=cn: h.dma_start(
                                out=VTOK[r0:r0 + 128, vo:vo + cn], in_=ot[:, 0:cn]),
                                reads=[ob], writes=[B_VTOK], sembuf=ob)
            P.barrier()

    def attn_core(ph_tiles, KT_ap, KT_buf, kbase, QT_ap, QT_buf, V_ap, V_buf, q0, nq, tiles, scale, banks_s, bank_o, bank_d,
                  cbias_ap, ptr, tmpr, mask=None):
        (po, pob), (pd, pdb) = bank_o, bank_d
        nt = len(tiles)
        for ti, tl in enumerate(tiles):
            pst, psb = banks_s[ti % len(banks_s)]
            qlo = tl["qlo"]
            has_mask = tl.get("mask") is not None
            P.op("tensor", lambda h, pst=pst, tl=tl, qlo=qlo, has_mask=has_mask: h.matmul(
                pst[:, qlo:nq], lhsT=KT_ap(tl["kt"]), rhs=QT_ap(qlo, nq), start=True, stop=not has_mask),
                reads=[KT_buf, QT_buf], writes=[psb], sig=not has_mask)
            if has_mask:
                mj, mlo, m_ap, m_buf = tl["mask"]
                P.op("tensor", lambda h, pst=pst, mj=mj, mlo=mlo, m_ap=m_ap: h.matmul(
                    pst[:, mlo:nq], lhsT=eall[:, mj * 128:(mj + 1) * 128], rhs=m_ap[:, mlo:nq], start=False, stop=True),
                    reads=[B_eall, m_buf], writes=[psb], sig=True)
            pt_, ptb = ptr.next()
            for (lo, hi, kind, bap, bbuf) in tl["bias"]:
                if kind == "const":
                    P.op("scalar", lambda h, pst=pst, pt_=pt_, lo=lo, hi=hi, bap=bap: h.activation(
                        out=pt_[:, lo:hi], in_=pst[:, lo:hi], func=AF.Exp, scale=scale, bias=bap),
                        [psb, bbuf], [ptb])
                else:
                    tt_, ttb = tmpr.next()
                    P.op("vector", lambda h, pst=pst, tt_=tt_, lo=lo, hi=hi, bap=bap: h.scalar_tensor_tensor(
                        out=tt_[:, lo:hi], in0=pst[:, lo:hi], scalar=scale, in1=bap, op0=ALU.mult, op1=ALU.add),
                        [psb, bbuf], [ttb])
                    P.op("scalar", lambda h, tt_=tt_, pt_=pt_, lo=lo, hi=hi: h.activation(
                        out=pt_[:, lo:hi], in_=tt_[:, lo:hi], func=AF.Exp), [ttb], [ptb])
            first, last = (ti == 0), (ti == nt - 1)
            P.op("tensor", lambda h, po=po, pt_=pt_, tl=tl, qlo=qlo, first=first, last=last: h.matmul(
                po[:, qlo:nq], lhsT=V_ap(tl["kt"]), rhs=pt_[:, qlo:nq], start=first, stop=last),
                reads=[V_buf, ptb], writes=[pob], sig=False)
            P.op("tensor", lambda h, pd=pd, pt_=pt_, qlo=qlo, first=first, last=last: h.matmul(
                pd[:, qlo:nq], lhsT=ones_b[:], rhs=pt_[:, qlo:nq], start=first, stop=last),
                reads=[B_ones_b, ptb], writes=[pdb], sig=True)

    def phase_B(l):
        with ExitStack() as ph:
            NQB = S // 512
            qtr = Ring([sb(ph, "QTh%d" % i, [128, S], BF16) for i in range(2)])
            ktr = Ring([sb(ph, "KTh%d" % i, [128, S], BF16) for i in range(2)])
            vr = Ring([sb(ph, "Vh%d" % i, [128, NKT, 128], BF16) for i in range(2)])
            ptr = Ring([sb(ph, "PT%d" % i, [128, 512], BF16) for i in range(4)])
            tmpr = Ring([sb(ph, "TMP%d" % i, [128, 512], F32) for i in range(3)])
            bdr = Ring([sb(ph, "BD%d" % i, [128, 256], F32) for i in range(2)])
            ostr = Ring([sb(ph, "ostB%d" % i, [128, 512], BF16) for i in range(3)])
            rdr = Ring([sb(ph, "RD%d" % i, [128, 512], F32) for i in range(3)])
            qtf, B_qtf = sb(ph, "QTf", [128, S], F32)
            kmean, B_kmean = sb(ph, "kmean", [128, NBLK], F32)
            pmask, B_pmask = sb(ph, "pmask", [128, NT * 4, 16], F32)
            gmr = Ring([sb(ph, "GM%d" % i, [128, 16], F32) for i in range(2)])
            t8r = Ring([sb(ph, "T8%d" % i, [128, 8], F32) for i in range(2)])
            mbr = Ring([sb(ph, "MB%d" % i, [128, 16], F32) for i in range(2)])
            mtr = Ring([sb(ph, "MT%d" % i, [16, 512], BF16) for i in range(2)])
            P.dma("sync", lambda h: h.dma_start(out=pmask[:], in_=pastmask[:]),
                  writes=[B_pmask], sembuf=B_pmask)

            def load_head(qch, kch, voff):
                qt, qb = qtr.next()
                kt, kb = ktr.next()
                vt, vb = vr.next()
                P.dma("sync", lambda h: h.dma_start(out=qt[:], in_=QKT[qch]), reads=[B_QKT], writes=[qb], sembuf=qb)
                P.dma("sync", lambda h: h.dma_start(out=kt[:], in_=QKT[kch]), reads=[B_QKT], writes=[kb], sembuf=kb)
                for j0 in range(0, NKT, 8):
                    j1 = min(NKT, j0 + 8)
                    P.dma("sync", lambda h, j0=j0, j1=j1: h.dma_start(
                        out=vt[:, j0:j1, :], in_=VTOK[j0 * 128:j1 * 128, voff:voff + 128].rearrange("(j p) d -> p j d", p=128)),
                        reads=[B_VTOK], writes=[vb], sembuf=vb)
                return (qt, qb), (kt, kb), (vt, vb)

            def load_bias(hglob):
                bt, bb = bdr.next()
                P.dma("sync", lambda h: h.dma_start(out=bt[:, 0:128], in_=bdiag[hglob]), writes=[bb], sembuf=bb)
                P.dma("sync", lambda h: h.dma_start(out=bt[:, 128:256], in_=bprev[hglob]), writes=[bb], sembuf=bb)
                return bt, bb

            def tiles_for_block(g, bt, bb, cb_ap, mask_ap=None, mask_buf=None):
                tl = []
                for j in range(4 * g):
                    d = {"kt": j, "qlo": 0}
                    if j == 4 * g - 1:
                        d["bias"] = [(0, 128, "matrix", bt[:, 128:256], bb), (128, 512, "const", cb_ap, B_c31)]
                    else:
                        d["bias"] = [(0, 512, "const", cb_ap, B_c31)]
                    if mask_ap is not None:
                        d["mask"] = (j // 2, 0, mask_ap, mask_buf)
                    tl.append(d)
                for u in range(4):
                    d = {"kt": 4 * g + u, "qlo": u * 128}
                    bl = []
                    if u < 3:
                        bl.append((u * 128, u * 128 + 256, "matrix", bt[:, 0:256], bb))
                    else:
                        bl.append((384, 512, "matrix", bt[:, 0:128], bb))
                    if u * 128 + 256 < 512:
                        bl.append((u * 128 + 256, 512, "const", cb_ap, B_c31))
                    d["bias"] = bl
                    if mask_ap is not None and u < 2:
                        d["mask"] = (2 * g, 256, mask_ap, mask_buf)
                    tl.append(d)
                return tl

            sc128 = 128.0 ** -0.5
            for hh in range(H_A):
                (qt, qb), (kt, kb), (vt, vb) = load_head(CH_QA + hh, CH_KA + hh, VOFF_A + hh * 128)
                bt, bb = load_bias(hh)
                P.op("scalar", lambda h, qt=qt: h.activation(out=qtf[:], in_=qt[:], func=AF.Copy), [qb], [B_qtf])
                P.op("vector", lambda h, kt=kt: h.tensor_reduce(
                    out=kmean[:], in_=kt[:].rearrange("p (j k) -> p j k", k=256), axis=AX.X, op=ALU.add), [kb], [B_kmean])
                P.op("vector", lambda h: h.tensor_scalar(out=kmean[:], in0=kmean[:], scalar1=1.0 / 256.0, scalar2=None, op0=ALU.mult),
                     [B_kmean], [B_kmean])
                for g in range(NQB):
                    q0 = g * 512
                    mt, mtb = mtr.next()
                    for s in range(4):
                        pg, pgb = ps[4 + (s % 2)]
                        P.op("tensor", lambda h, pg=pg, qq=q0 + s * 128: h.matmul(
                            pg[:, 0:NBLK], lhsT=qtf[:, qq:qq + 128], rhs=kmean[:, :], start=True, stop=True),
                            reads=[B_qtf, B_kmean], writes=[pgb], sig=True)
                        gm, gmb = gmr.next()
                        t8, t8b = t8r.next()
                        mb_, mbb = mbr.next()
                        if NBLK < 16:
                            P.op("vector", lambda h, gm=gm: h.memset(gm[:], NEG), [], [gmb])
                        P.op("vector", lambda h, gm=gm, pg=pg, idx=g * 4 + s: h.tensor_tensor(
                            out=gm[:, 0:NBLK], in0=pg[:, 0:NBLK], in1=pmask[:, idx, 0:NBLK], op=ALU.add), [pgb, B_pmask], [gmb])
                        P.op("vector", lambda h, gm=gm, t8=t8: h.max(out=t8[:], in_=gm[:]), [gmb], [t8b])
                        P.op("vector", lambda h, gm=gm, t8=t8, mb_=mb_: h.tensor_scalar(
                            out=mb_[:], in0=gm[:], scalar1=t8[:, 2:3], scalar2=MNEG, op0=ALU.is_ge, op1=ALU.mult), [gmb, t8b], [mbb])
                        ptp, ptpb = ps[6 + (s % 2)]
                        P.op("tensor", lambda h, ptp=ptp, mb_=mb_: h.transpose(out=ptp[0:16, 0:128], in_=mb_[:], identity=ident[:]),
                             reads=[mbb, B_ident], writes=[ptpb], sig=True)
                        P.op("vector", lambda h, mt=mt, ptp=ptp, s=s: h.tensor_scalar(
                            out=mt[:, s * 128:(s + 1) * 128], in0=ptp[0:16, 0:128], scalar1=-MNEG, scalar2=None, op0=ALU.add),
                            [ptpb], [mtb])
                    tiles = tiles_for_block(g, bt, bb, c31[:, hh:hh + 1], mt, mtb)
                    attn_core(None, lambda j, kt=kt: kt[:, j * 128:(j + 1) * 128], kb, 0,
                              lambda lo, hi, qt=qt, q0=q0: qt[:, q0 + lo:q0 + hi], qb,
                              lambda j, vt=vt: vt[:, j, :], vb, q0, 512, tiles, sc128,
                              [ps[0], ps[1]], ps[2], ps[3], None, ptr, tmpr)
                    rd, rdb = rdr.next()
                    ot, ob = ostr.next()
                    P.op("vector", lambda h, rd=rd: h.reciprocal(out=rd[:], in_=ps[3][0][:]), [ps[3][1]], [rdb])
                    P.op("vector", lambda h, rd=rd, ot=ot: h.tensor_tensor(out=ot[:], in0=ps[2][0][:], in1=rd[:], op=ALU.mult),
                         [ps[2][1], rdb], [ob])
                    P.dma("gpsimd", lambda h, ot=ot, ch=hh, q0=q0: h.dma_start(out=OT[ch, :, q0:q0 + 512], in_=ot[:]),
                          reads=[ob], writes=[B_OT], sembuf=ob)

            sc64 = 64.0 ** -0.5
            o0r = Ring([sb(ph, "O0_%d" % i, [128, 512], F32) for i in range(2)])
            for hh in range(H_C):
                (qt, qb), (kt, kb), (vt, vb) = load_head(CH_QC + hh, CH_KC + hh, VOFF_C + hh * 128)
                bt, bb = load_bias(H_A + H_B + hh)
                cb = c31[:, H_A + H_B + hh:H_A + H_B + hh + 1]
                for g in range(NQB):
                    q0 = g * 512
                    tiles = tiles_for_block(g, bt, bb, cb)
                    res = []
                    for m in range(2):
                        plo, phi = m * 64, (m + 1) * 64
                        attn_core(None, lambda j, kt=kt, plo=plo, phi=phi: kt[plo:phi, j * 128:(j + 1) * 128], kb, 0,
                                  lambda lo, hi, qt=qt, q0=q0, plo=plo, phi=phi: qt[plo:phi, q0 + lo:q0 + hi], qb,
                                  lambda j, vt=vt: vt[:, j, :], vb, q0, 512, tiles, sc64,
                                  [ps[0], ps[1]], ps[2 + 2 * m], ps[3 + 2 * m], None, ptr, tmpr)
                        rd, rdb = rdr.next()
                        om, omb = o0r.next()
                        P.op("vector", lambda h, rd=rd, m=m: h.reciprocal(out=rd[:], in_=ps[3 + 2 * m][0][:]), [ps[3 + 2 * m][1]], [rdb])
                        P.op("vector", lambda h, rd=rd, om=om, m=m: h.tensor_tensor(
                            out=om[:], in0=ps[2 + 2 * m][0][:], in1=rd[:], op=ALU.mult), [ps[2 + 2 * m][1], rdb], [omb])
                        res.append((om, omb))
                    (o0, o0b), (o1, o1b) = res
                    P.op("vector", lambda h, o0=o0, o1=o1: h.scalar_tensor_tensor(
                        out=o0[:], in0=o1[:], scalar=lamv[:, l, 1:2], in1=o0[:], op0=ALU.mult, op1=ALU.add),
                        [o0b, o1b, B_lamv], [o0b])
                    P.op("scalar", lambda h, o0=o0, o1=o1: h.activation(out=o1[:], in_=o0[:], func=AF.Square), [o0b], [o1b])
                    pss, pssb = ps[6]
                    P.op("tensor", lambda h, pss=pss, o1=o1: h.matmul(pss[:], lhsT=ones_f[:], rhs=o1[:], start=True, stop=True),
                         reads=[B_ones_f, o1b], writes=[pssb], sig=True)
                    rd, rdb = rdr.next()
                    P.op("scalar", lambda h, rd=rd, pss=pss: h.activation(out=rd[:], in_=pss[:], func=AF.Sqrt, scale=1.0 / 128.0, bias=epsv[:, 0:1]),
                         [pssb, B_epsv], [rdb])
                    P.op("vector", lambda h, rd=rd: h.reciprocal(out=rd[:], in_=rd[:]), [rdb], [rdb])
                    ot, ob = ostr.next()
                    P.op("vector", lambda h, o0=o0, rd=rd, ot=ot: h.scalar_tensor_tensor(
                        out=ot[:], in0=o0[:], scalar=lamv[:, l, 2:3], in1=rd[:], op0=ALU.mult, op1=ALU.mult),
                        [o0b, rdb, B_lamv], [ob])
                    P.dma("gpsimd", lambda h, ot=ot, ch=H_A + H_B + hh, q0=q0: h.dma_start(out=OT[ch, :, q0:q0 + 512], in_=ot[:]),
                          reads=[ob], writes=[B_OT], sembuf=ob)

            bsw, B_bsw = sb(ph, "bsw", [128, 2, H_B, 128], F32)
            sinke, B_sinke = sb(ph, "sinke", [128, H_B, 128], F32)
            for hq in range(H_B):
                P.op("vector", lambda h, hq=hq: h.tensor_scalar(
                    out=sinke[:, hq, :], in0=ones_f[:], scalar1=sexp[:, l, hq:hq + 1], scalar2=None, op0=ALU.mult),
                    [B_ones_f, B_sexp], [B_sinke])
            P.dma("sync", lambda h: h.dma_start(out=bsw[:, 0, :, :], in_=bdiag[H_A:H_A + H_B].rearrange("h k q -> k h q")),
                  writes=[B_bsw], sembuf=B_bsw)
            P.dma("sync", lambda h: h.dma_start(out=bsw[:, 1, :, :], in_=bprev[H_A:H_A + H_B].rearrange("h k q -> k h q")),
                  writes=[B_bsw], sembuf=B_bsw)
            qbr = Ring([sb(ph, "QB%d" % i, [128, 8, 512], BF16) for i in range(2)])
            ostw = Ring([sb(ph, "ostW%d" % i, [128, 4, 128], BF16) for i in range(3)])
            for kv in range(KV_B):
                _q, (kt, kb), (vt, vb) = load_head(CH_KB + kv, CH_KB + kv, VOFF_B + kv * 128)
                for g in range(NQB):
                    q0 = g * 512
                    qt, qb = qbr.next()
                    P.dma("sync", lambda h, qt=qt, kv=kv, q0=q0: h.dma_start(
                        out=qt[:], in_=QKT[CH_QB + kv * 8:CH_QB + kv * 8 + 8, :, q0:q0 + 512].rearrange("c p s -> p c s")),
                        reads=[B_QKT], writes=[qb], sembuf=qb)
                    for s in range(4):
                        jt = 4 * g + s
                        for hg in range(2):
                            h0 = hg * 4
                            hb0 = kv * 8 + h0
                            po, pob = ps[2 + hg]
                            pd, pdb = ps[4 + hg]
                            klist = ([jt - 1] if jt > 0 else []) + [jt]
                            for ki, kj in enumerate(klist):
                                pst, psb = ps[ki]
                                which = 0 if kj == jt else 1
                                P.op("tensor", lambda h, pst=pst, kt=kt, kj=kj, qt=qt, h0=h0, s=s: h.matmul(
                                    pst[:].rearrange("p (a q) -> p a q", a=4), lhsT=kt[:, kj * 128:(kj + 1) * 128],
                                    rhs=qt[:, h0:h0 + 4, s * 128:(s + 1) * 128], start=True, stop=True),
                                    reads=[kb, qb], writes=[psb], sig=True)
                                tt_, ttb = tmpr.next()
                                pt_, ptb = ptr.next()
                                P.op("vector", lambda h, pst=pst, tt_=tt_, which=which, hb0=hb0: h.scalar_tensor_tensor(
                                    out=tt_[:].rearrange("p (a q) -> p a q", a=4), in0=pst[:].rearrange("p (a q) -> p a q", a=4),
                                    scalar=sc128, in1=bsw[:, which, hb0:hb0 + 4, :], op0=ALU.mult, op1=ALU.add),
                                    [psb, B_bsw], [ttb])
                                P.op("scalar", lambda h, tt_=tt_, pt_=pt_: h.activation(out=pt_[:], in_=tt_[:], func=AF.Exp), [ttb], [ptb])
                                first, last = (ki == 0), (ki == len(klist) - 1)
                                P.op("tensor", lambda h, po=po, vt=vt, kj=kj, pt_=pt_, first=first, last=last: h.matmul(
                                    po[:], lhsT=vt[:, kj, :], rhs=pt_[:], start=first, stop=last),
                                    reads=[vb, ptb], writes=[pob], sig=False)
                                P.op("tensor", lambda h, pd=pd, pt_=pt_, first=first, last=last: h.matmul(
                                    pd[:], lhsT=ones_b[:], rhs=pt_[:], start=first, stop=last),
                                    reads=[B_ones_b, ptb], writes=[pdb], sig=True)
                            rd, rdb = rdr.next()
                            P.op("vector", lambda h, rd=rd, pd=pd, hb0=hb0: h.tensor_tensor(
                                out=rd[:].rearrange("p (a q) -> p a q", a=4), in0=pd[:].rearrange("p (a q) -> p a q", a=4),
                                in1=sinke[:, hb0:hb0 + 4, :], op=ALU.add), [pdb, B_sinke], [rdb])
                            P.op("vector", lambda h, rd=rd: h.reciprocal(out=rd[:], in_=rd[:]), [rdb], [rdb])
                            ot, ob = ostw.next()
                            P.op("vector", lambda h, ot=ot, po=po, rd=rd: h.tensor_tensor(
                                out=ot[:], in0=po[:].rearrange("p (a q) -> p a q", a=4), in1=rd[:].rearrange("p (a q) -> p a q", a=4),
                                op=ALU.mult), [pob, rdb], [ob])
                            ch0 = H_A + hb0
                            P.dma("gpsimd", lambda h, ot=ot, ch0=ch0, qq=q0 + s * 128: h.dma_start(
                                out=OT[ch0:ch0 + 4, :, qq:qq + 128].rearrange("c p s -> p c s"), in_=ot[:]),
                                reads=[ob], writes=[B_OT], sembuf=ob)
            P.barrier()

    def layer_norm_tile(r, B_r, sqr, stat, l, which, out_fn):
        (mean, B_mean), (rstd, B_rstd), (msq, B_msq) = stat
        p1, p1b = ps[6]
        p2, p2b = ps[7]
        for n in range(KC):
            sq, sqb = sqr.next()
            P.op("scalar", lambda h, sq=sq, n=n: h.activation(out=sq[:], in_=r[:, n, :], func=AF.Square), [B_r[n]], [sqb])
            P.op("tensor", lambda h, n=n: h.matmul(p1[:], lhsT=ones_f[:], rhs=r[:, n, :], start=(n == 0), stop=(n == KC - 1)),
                 reads=[B_ones_f, B_r[n]], writes=[p1b], sig=False)
            P.op("tensor", lambda h, sq=sq, n=n: h.matmul(p2[:], lhsT=ones_f[:], rhs=sq[:], start=(n == 0), stop=(n == KC - 1)),
                 reads=[B_ones_f, sqb], writes=[p2b], sig=True)
        P.op("scalar", lambda h: h.activation(out=mean[:], in_=p1[:], func=AF.Copy, scale=1.0 / D), [p1b], [B_mean])
        P.op("vector", lambda h: h.tensor_tensor(out=msq[:], in0=mean[:], in1=mean[:], op=ALU.mult), [B_mean], [B_msq])
        P.op("vector", lambda h: h.scalar_tensor_tensor(out=msq[:], in0=p2[:], scalar=1.0 / D, in1=msq[:], op0=ALU.mult, op1=ALU.subtract),
             [p2b, B_msq], [B_msq])
        P.op("scalar", lambda h: h.activation(out=rstd[:], in_=msq[:], func=AF.Sqrt, bias=epsv[:, 0:1]), [B_msq, B_epsv], [B_rstd])
        P.op("vector", lambda h: h.reciprocal(out=rstd[:], in_=rstd[:]), [B_rstd], [B_rstd])
        gi = l * 2 + which
        for n in range(KC):
            P.op("vector", lambda h, n=n: h.tensor_tensor(out=r[:, n, :], in0=r[:, n, :], in1=mean[:], op=ALU.subtract), [B_r[n], B_mean], [B_r[n]])
            P.op("vector", lambda h, n=n: h.tensor_tensor(out=r[:, n, :], in0=r[:, n, :], in1=rstd[:], op=ALU.mult), [B_r[n], B_rstd], [B_r[n]])
            P.op("scalar", lambda h, n=n: h.activation(out=r[:, n, :], in_=r[:, n, :], func=AF.Identity,
                                                       scale=lng[:, gi, n:n + 1], bias=lnb[:, gi, n:n + 1]), [B_r[n], B_lng, B_lnb], [B_r[n]])
            out_fn(n)

    def phase_C(l, xsrc, B_xsrc, xdst, B_xdst):
        with ExitStack() as ph:
            r, B_r = sbc(ph, "rC", [128, KC, TT], F32, KC)
            a32, B_a32 = sbc(ph, "a32", [128, KC, TT], BF16, KC)
            act, B_act = sbc(ph, "actC", [128, 30, TT], BF16, 30)
            wring = Ring([sb(ph, "wC%d" % i, [128, 8, 512], BF16) for i in range(3)])
            xr = Ring([sb(ph, "xC%d" % i, [128, TT], F32) for i in range(3)])
            sqr = Ring([sb(ph, "sqC%d" % i, [128, TT], F32) for i in range(2)])
            sgr = Ring([sb(ph, "sgC%d" % i, [128, TT], F32) for i in range(2)])
            stat = [sb(ph, "st_mean", [128, TT], F32), sb(ph, "st_rstd", [128, TT], F32), sb(ph, "st_msq", [128, TT], F32)]
            ostr = Ring([sb(ph, "ostC%d" % i, [128, TT], F32) for i in range(2)])
            g1 = modv[:, l, 64:96]
            sh2 = modv[:, l, 96:128]
            sc2p = modv[:, l, 128:160]
            g2 = modv[:, l, 160:192]
            for t in range(NT):
                t0 = t * TT
                for c0 in range(0, KC, 8):
                    P.dma("sync", lambda h, t0=t0, c0=c0: h.dma_start(
                        out=a32[:, c0:c0 + 8, :], in_=OT[c0:c0 + 8, :, t0:t0 + TT].rearrange("c p s -> p c s")),
                        reads=[B_OT], writes=B_a32[c0:c0 + 8], sembuf=B_a32[c0])
                ws = WStream(wring)
                plan = []
                for nb in range(KC // 4):
                    plan.append([ws.add([(("o", l), kg * 8, 8, nb * 512, 512, 0)]) for kg in range(4)])
                bank = 0
                for nb, jobs in enumerate(plan):
                    banks = [ps[(bank + i) % 4] for i in range(4)] if False else [ps[(bank + i) % 8] for i in range(4)]
                    bank = (bank + 4) % 8
                    if bank == 0 and False:
                        pass
                    for kg, j in enumerate(jobs):
                        wt, wb = ws.get(j)
                        for k in range(8):
                            kk = kg * 8 + k
                            for i in range(4):
                                pt, pb = banks[i]
                                P.op("tensor", lambda h, pt=pt, wt=wt, k=k, i=i, kk=kk: h.matmul(
                                    pt[:], lhsT=wt[:, k, i * 128:(i + 1) * 128], rhs=a32[:, kk, :], start=(kk == 0), stop=(kk == KC - 1)),
                                    reads=[wb, B_a32[kk]], writes=[pb], sig=(kk == KC - 1) or (k == 7 and i == 3))
                    for i in range(4):
                        n = nb * 4 + i
                        pt, pb = banks[i]
                        xt, xb = xr.next()
                        P.dma("sync", lambda h, xt=xt, n=n, t0=t0: h.dma_start(out=xt[:], in_=xsrc[n, :, t0:t0 + TT]),
                              reads=[B_xsrc], writes=[xb], sembuf=xb)
                        P.op("scalar", lambda h, xt=xt: h.mul(xt[:], xt[:], ALPHA), [xb], [xb])
                        P.op("vector", lambda h, pt=pt, xt=xt, n=n: h.scalar_tensor_tensor(
                            out=r[:, n, :], in0=pt[:], scalar=g1[:, n:n + 1], in1=xt[:], op0=ALU.mult, op1=ALU.add),
                            [pb, xb, B_modv], [B_r[n]])

                def mk_h2(n):
                    P.op("vector", lambda h, n=n: h.tensor_scalar(
                        out=a32[:, n, :], in0=r[:, n, :], scalar1=sc2p[:, n:n + 1], scalar2=sh2[:, n:n + 1], op0=ALU.mult, op1=ALU.add),
                        [B_r[n], B_modv], [B_a32[n]])
                layer_norm_tile(r, B_r, sqr, stat, l, 0, mk_h2)
                for hi_, (f0, f1) in enumerate(FF_HALVES):
                    nf = f1 - f0
                    ws = WStream(wring)
                    gu_plan = []
                    for j2 in range(nf // 2):
                        c0 = (f0 + 2 * j2) * 128
                        gu_plan.append([ws.add([(("gate", l), kg * 8, 8, c0, 256, 0), (("up", l), kg * 8, 8, c0, 256, 256)])
                                        for kg in range(4)])
                    dn_plan = []
                    kgs = []
                    k = 0
                    while k < nf:
                        kgs.append((k, min(8, nf - k)))
                        k += 8
                    for nb in range(KC // 4):
                        dn_plan.append([ws.add([(("down", l), f0 + k0, kn, nb * 512, 512, 0)]) for (k0, kn) in kgs])
                    bank = 0
                    for j2, jobs in enumerate(gu_plan):
                        banks = [ps[(bank + i) % 8] for i in range(4)]
                        bank = (bank + 4) % 8
                        for kg, j in enumerate(jobs):
                            wt, wb = ws.get(j)
                            for k in range(8):
                                kk = kg * 8 + k
                                for i in range(4):
                                    pt, pb = banks[i]
                                    P.op("tensor", lambda h, pt=pt, wt=wt, k=k, i=i, kk=kk: h.matmul(
                                        pt[:], lhsT=wt[:, k, i * 128:(i + 1) * 128], rhs=a32[:, kk, :], start=(kk == 0), stop=(kk == KC - 1)),
                                        reads=[wb, B_a32[kk]], writes=[pb], sig=(kk == KC - 1) or (k == 7 and i == 3))
                        for i in range(2):
                            sg, sgb = sgr.next()
                            (pg, pgb), (pu, pub) = banks[i], banks[2 + i]
                            P.op("scalar", lambda h, sg=sg, pg=pg: h.activation(out=sg[:], in_=pg[:], func=AF.Silu), [pgb], [sgb])
                            P.op("vector", lambda h, sg=sg, pu=pu, jj=2 * j2 + i: h.tensor_tensor(
                                out=act[:, jj, :], in0=pu[:], in1=sg[:], op=ALU.mult), [pub, sgb], [B_act[2 * j2 + i]])
                    for nb, jobs in enumerate(dn_plan):
                        banks = [ps[(bank + i) % 8] for i in range(4)]
                        bank = (bank + 4) % 8
                        for (k0, kn), j in zip(kgs, jobs):
                            wt, wb = ws.get(j)
                            for k in range(kn):
                                kk = k0 + k
                                for i in range(4):
                                    pt, pb = banks[i]
                                    P.op("tensor", lambda h, pt=pt, wt=wt, k=k, i=i, kk=kk, nf=nf, kn=kn: h.matmul(
                                        pt[:], lhsT=wt[:, k, i * 128:(i + 1) * 128], rhs=act[:, kk, :], start=(kk == 0), stop=(kk == nf - 1)),
                                        reads=[wb, B_act[kk]], writes=[pb], sig=(kk == nf - 1) or (k == kn - 1 and i == 3))
                        for i in range(4):
                            n = nb * 4 + i
                            pt, pb = banks[i]
                            if hi_ == 0:
                                P.op("scalar", lambda h, n=n: h.mul(r[:, n, :], r[:, n, :], ALPHA), [B_r[n]], [B_r[n]])
                            P.op("vector", lambda h, pt=pt, n=n: h.scalar_tensor_tensor(
                                out=r[:, n, :], in0=pt[:], scalar=g2[:, n:n + 1], in1=r[:, n, :], op0=ALU.mult, op1=ALU.add),
                                [pb, B_r[n], B_modv], [B_r[n]])

                def mk_out(n, t0=t0):
                    ot, ob = ostr.next()
                    P.op("vector", lambda h, ot=ot, n=n: h.tensor_copy(out=ot[:], in_=r[:, n, :]), [B_r[n]], [ob])
                    P.dma("gpsimd", lambda h, ot=ot, n=n, t0=t0: h.dma_start(out=xdst[n, :, t0:t0 + TT], in_=ot[:]),
                          reads=[ob], writes=[B_xdst], sembuf=ob)
                layer_norm_tile(r, B_r, sqr, stat, l, 1, mk_out)
            P.barrier()

    epsv, B_epsv = sb(st, "epsv", [128, 1], F32)
    P.op("vector", lambda h: h.memset(epsv[:], LN_EPS), writes=[B_epsv])

    def phase_scalars():
        with ExitStack() as ph:
            lr, B_lr = sb(ph, "lr", [128, depth, 256], F32)
            sr_, B_sr = sexp, B_sexp
            tmp, B_tmp = sb(ph, "lt", [128, depth, 2, 64], F32)
            red, B_red = sb(ph, "lred", [128, depth, 2], F32)
            P.dma("sync", lambda h: h.dma_start(out=lr[:], in_=lam_rep.rearrange("l p k -> p l k")), writes=[B_lr], sembuf=B_lr)
            P.dma("sync", lambda h: h.dma_start(out=sr_[:], in_=sinks_rep.rearrange("l p k -> p l k")), writes=[B_sr], sembuf=B_sr)
            for l in range(depth):
                lam_init = 0.8 - 0.6 * math.exp(-0.3 * l)
                lv = lr[:, l, :].rearrange("p (a two k) -> p a two k", a=2, two=2)
                P.op("vector", lambda h, lv=lv, l=l: h.tensor_tensor(
                    out=tmp[:, l, :, :], in0=lv[:, :, 0, :], in1=lv[:, :, 1, :], op=ALU.mult), [B_lr], [B_tmp])
                P.op("vector", lambda h, l=l: h.tensor_reduce(out=red[:, l, :], in_=tmp[:, l, :, :], axis=AX.X, op=ALU.add), [B_tmp], [B_red])
                P.op("scalar", lambda h, l=l: h.activation(out=red[:, l, :], in_=red[:, l, :], func=AF.Exp), [B_red], [B_red])
                P.op("vector", lambda h, l=l, lam_init=lam_init: h.scalar_tensor_tensor(
                    out=lamv[:, l, 0:1], in0=red[:, l, 0:1], scalar=lam_init, in1=red[:, l, 1:2], op0=ALU.add, op1=ALU.subtract),
                    [B_red], [B_lamv])
                P.op("vector", lambda h, l=l: h.tensor_scalar(out=lamv[:, l, 1:2], in0=lamv[:, l, 0:1], scalar1=-1.0, scalar2=None, op0=ALU.mult),
                     [B_lamv], [B_lamv])
                P.op("vector", lambda h, l=l, lam_init=lam_init: h.tensor_scalar(
                    out=lamv[:, l, 2:3], in0=subg[:, l:l + 1], scalar1=1.0 - lam_init, scalar2=None, op0=ALU.mult), [B_subg], [B_lamv])
                P.op("scalar", lambda h, l=l: h.activation(out=sr_[:, l, :], in_=sr_[:, l, :], func=AF.Exp), [B_sr], [B_sr])
            P.barrier()

    if "s" in phases:
        phase_scalars()
    if "w" in phases:
        phase_weights(0)
    if "m" in phases:
        phase_mod()
    if depth > 1 and "w" in phases:
        phase_weights(1)
    for l in range(depth):
        xsrc, bx = (xT, Buf("xT")) if l == 0 else (X1T, B_X1T)
        xdst, bd = (outT, B_out) if l == depth - 1 else (X1T, B_X1T)
        if "A" in phases:
            phase_A(l, xsrc, bx)
        if "B" in phases:
            phase_B(l)
        if "C" in phases:
            phase_C(l, xsrc, bx, xdst, bd)
    if dbg:
        dbg_mod = nc.dram_tensor("dbg_mod", [128, depth * 192], F32, kind="ExternalOutput").ap()
        P.dma("sync", lambda h: h.dma_start(out=dbg_mod[:], in_=modv[:].rearrange("p l c -> p (l c)")), reads=[B_modv], sembuf=B_modv)
        dbg_lam = nc.dram_tensor("dbg_lam", [128, depth * 4], F32, kind="ExternalOutput").ap()
        P.dma("sync", lambda h: h.dma_start(out=dbg_lam[:], in_=lamv[:].rearrange("p l c -> p (l c)")), reads=[B_lamv], sembuf=B_lamv)
        for nm, t in (("QKT", QKT), ("VTOK", VTOK), ("OT", OT)):
            shp = list(t.shape)
            d = nc.dram_tensor("dbg_" + nm, shp, BF16, kind="ExternalOutput").ap()
            bb = {"QKT": B_QKT, "VTOK": B_VTOK, "OT": B_OT}[nm]
            if nm == "VTOK":
                for i in range(shp[0] // 128):
                    P.dma("sync", lambda h, d=d, t=t, i=i: h.dma_start(out=d[i * 128:(i + 1) * 128, :], in_=t[i * 128:(i + 1) * 128, :]),
                          reads=[bb], sembuf=Buf("dbgcp_" + nm))
            else:
                for i in range(shp[0]):
                    P.dma("sync", lambda h, d=d, t=t, i=i: h.dma_start(out=d[i], in_=t[i]), reads=[bb], sembuf=Buf("dbgcp_" + nm))
    print("n_ops", P.n_ops, "n_sems", len(P.sems))
    P.barrier()
    P.emit(block)
    st.close()
    return nc


def host_constants(S, rel_bias):
    k = np.arange(128)[:, None]
    q = np.arange(128)[None, :]
    d_diag = q - k
    d_prev = 128 + q - k
    bd = np.empty((32, 128, 128), np.float32)
    bp = np.empty((32, 128, 128), np.float32)
    bk_d = rel_bucket_np(d_diag)
    bk_p = rel_bucket_np(d_prev)
    for h in range(32):
        bd[h] = np.where(d_diag >= 0, rel_bias[bk_d, h], np.float32(NEG))
        v = rel_bias[bk_p, h]
        if H_A <= h < H_A + H_B:
            v = np.where(d_prev < 128, v, np.float32(NEG))
        bp[h] = v
    c31 = np.ascontiguousarray(np.broadcast_to(rel_bias[31][None, :], (128, 32))).astype(np.float32)
    NT = S // TT
    pm = np.zeros((128, NT * 4, 16), np.float32)
    for g in range(NT):
        for s in range(4):
            qblk = 2 * g + s // 2
            pm[:, g * 4 + s, qblk:] = NEG
    e = np.zeros((16, 16, 128), np.float32)
    for j in range(16):
        e[j, j, :] = 1.0
    import ml_dtypes
    e = e.reshape(16, 16 * 128).astype(ml_dtypes.bfloat16)
    return {"bdiag": bd, "bprev": bp, "c31_rep": c31, "pastmask": pm, "e_all": e, "ident": np.eye(128, dtype=np.float32)}


def host_inputs(S, core_batch, x, c, rel_bias, w_ada, b_ada, w_in, w_o, attn_sinks, diff_lambda, diff_subln_g,
                ln_g, ln_b, w_gate, w_up, w_down, depth=DEPTH):
    DEPTH = depth
    (w_ada, b_ada, w_in, w_o, attn_sinks, diff_lambda, diff_subln_g, ln_g, ln_b, w_gate, w_up, w_down) = [
        a[:depth] for a in (w_ada, b_ada, w_in, w_o, attn_sinks, diff_lambda, diff_subln_g, ln_g, ln_b, w_gate, w_up, w_down)]
    consts = host_constants(S, rel_bias)
    shared = {
        "b_ada_t": np.ascontiguousarray(b_ada.reshape(DEPTH, 6 * KC, 128).transpose(0, 2, 1)),
        "lng_t": np.ascontiguousarray(ln_g.reshape(DEPTH, 2, KC, 128).transpose(0, 1, 3, 2)),
        "lnb_t": np.ascontiguousarray(ln_b.reshape(DEPTH, 2, KC, 128).transpose(0, 1, 3, 2)),
        "sinks_rep": np.ascontiguousarray(np.broadcast_to(attn_sinks[:, None, :], (DEPTH, 128, H_B))),
        "lam_rep": np.ascontiguousarray(np.broadcast_to(diff_lambda.reshape(DEPTH, 1, 256), (DEPTH, 128, 256))),
        "subg_t": np.ascontiguousarray(diff_subln_g.T),
    }
    shared.update(consts)
    xT_cache = {}
    maps = []
    for r in range(NCORES):
        b = core_batch[r]
        if b not in xT_cache:
            xT_cache[b] = np.ascontiguousarray(x[b].T).reshape(KC, 128, S)
        m = dict(shared)
        m["xT"] = xT_cache[b]
        m["cTs"] = np.ascontiguousarray(c[:, r * 512:(r + 1) * 512].reshape(4, 4, 128).transpose(2, 1, 0))
        bs = np.zeros((4, 2), np.float32)
        bs[b, :] = 1.0
        m["bsel"] = bs
        m["w_ada_s"] = np.ascontiguousarray(w_ada[:, r * 512:(r + 1) * 512, :])
        m["w_in_s"] = np.ascontiguousarray(w_in[:, r * 512:(r + 1) * 512, :])
        m["w_o_s"] = np.ascontiguousarray(w_o[:, r * 512:(r + 1) * 512, :])
        m["w_gate_s"] = np.ascontiguousarray(w_gate[:, r * 512:(r + 1) * 512, :])
        m["w_up_s"] = np.ascontiguousarray(w_up[:, r * 512:(r + 1) * 512, :])
        m["w_down_s"] = np.ascontiguousarray(w_down[:, r * 1376:(r + 1) * 1376, :])
        maps.append(m)
    return maps


def kernel(x, c, rel_bias, w_ada, b_ada, w_in, w_o, attn_sinks, diff_lambda, diff_subln_g,
           ln_g, ln_b, w_gate, w_up, w_down):
    args = [np.asarray(a, dtype=np.float32) for a in (x, c, rel_bias, w_ada, b_ada, w_in, w_o, attn_sinks, diff_lambda,
                                                      diff_subln_g, ln_g, ln_b, w_gate, w_up, w_down)]
    x = args[0]
    B, S, _ = x.shape
    core_batch = [r // 2 for r in range(NCORES)]
    maps = host_inputs(S, core_batch, *args)
    nc = build_program(S)
    res = run_bass_kernel_spmd(nc, maps, core_ids=list(range(NCORES)))
    out = np.empty((B, S, D), np.float32)
    for b in range(B):
        out[b] = res.results[2 * b]["outT"].reshape(D, S).T
    return out
```

```python
import math
from contextlib import ExitStack

import numpy as np
import concourse.bass as bass
import concourse.mybir as mybir
from concourse.bass_utils import run_bass_kernel_spmd

F32 = mybir.dt.float32
BF16 = mybir.dt.bfloat16
AF = mybir.ActivationFunctionType
ALU = mybir.AluOpType
AX = mybir.AxisListType

D = 4096
KC = D // 128
DEPTH = 2
HD = 128
H_A, H_B, H_C, KV_B = 8, 16, 8, 2
D_PROJ = 8704
D_FF = 11008
FC = D_FF // 128
FF_HALVES = ((0, 30), (30, 58), (58, 86))
N_BUCKETS = 32
MAX_DISTANCE = 128
ALPHA = (2.0 * DEPTH) ** 0.25
LN_EPS = 1e-5
NEG = -1e30
MNEG = 30000.0
NCORES = 8
CH_QA, CH_KA, CH_VA, CH_QB, CH_KB, CH_VB, CH_QC, CH_KC, CH_VC = 0, 8, 16, 24, 40, 42, 44, 52, 60
V_COLS = 2304
VOFF_A, VOFF_B, VOFF_C = 0, 1024, 1280
TT = 512


class Buf:
    __slots__ = ("name", "last_w", "readers")

    def __init__(self, name):
        self.name = name
        self.last_w = {}
        self.readers = {}


class Eng:
    def __init__(self, name, lazy):
        self.name = name
        self.count = 0
        self.pending = False
        self.waited = {}
        self.ops = []
        self.lazy = lazy


class Prog:
    ENGS = ("tensor", "vector", "scalar", "gpsimd", "sync")

    def __init__(self, nc, stack):
        self.nc = nc
        self.eng = {n: Eng(n, lazy=(n == "tensor")) for n in self.ENGS}
        self.sems = {}
        self._stack = stack
        self.latest = {}
        self.semcount = {}
        self.n_ops = 0
        for n in self.ENGS:
            self.sems["E_" + n] = stack.enter_context(nc.semaphore("E_" + n))

    def _sem_for(self, buf, prefix):
        key = prefix + buf.name
        if key not in self.sems:
            self.sems[key] = self._stack.enter_context(self.nc.semaphore(key))
            self.semcount[key] = 0
        return key

    def _need(self, e, toks, skip=None):
        best = {}
        for k, v in toks:
            if k == skip:
                continue
            if e.waited.get(k, 0) >= v:
                continue
            if best.get(k, 0) < v:
                best[k] = v
        for k, v in best.items():
            e.waited[k] = v
            e.ops.append(("wait", self.sems[k], v))

    @staticmethod
    def _deps(reads, writes):
        toks = []
        for b in reads:
            toks.extend(b.last_w.items())
        for b in writes:
            toks.extend(b.last_w.items())
            toks.extend(b.readers.items())
        return toks

    def _commit(self, tok, reads, writes):
        k, v = tok
        for b in reads:
            if b.readers.get(k, 0) < v:
                b.readers[k] = v
        for b in writes:
            b.last_w[k] = v
            b.readers = {}
        if self.latest.get(k, 0) < v:
            self.latest[k] = v
        self.n_ops += 1

    def op(self, engine, fn, reads=(), writes=(), sig=True):
        e = self.eng[engine]
        ekey = "E_" + engine
        self._need(e, self._deps(reads, writes), skip=(ekey if engine == "tensor" else None))
        if sig:
            e.count += 1
            tok = (ekey, e.count)
            e.ops.append(("op", fn, self.sems[ekey]))
        else:
            tok = (ekey, e.count + 1)
            e.ops.append(("op", fn, None))
        e.pending = not sig
        self._commit(tok, reads, writes)
        return tok

    def dma(self, queue, fn, reads=(), writes=(), sembuf=None):
        e = self.eng[queue]
        key = self._sem_for(sembuf, "D_")
        self._need(e, self._deps(reads, writes))
        self.semcount[key] += 16
        tok = (key, self.semcount[key])
        e.ops.append(("dma", fn, self.sems[key]))
        self._commit(tok, reads, writes)
        return tok

    def cc(self, fn, reads=(), writes=(), sembuf=None):
        e = self.eng["gpsimd"]
        key = self._sem_for(sembuf, "C_")
        self._need(e, self._deps(reads, writes))
        self.semcount[key] += 1
        tok = (key, self.semcount[key])
        e.ops.append(("cc", fn, self.sems[key]))
        self._commit(tok, reads, writes)
        return tok

    def barrier(self):
        pe = self.eng["tensor"]
        if pe.pending:
            raise RuntimeError("barrier with unsignaled PE op pending")
        toks = list(self.latest.items())
        for n in self.ENGS:
            self._need(self.eng[n], toks)

    def emit(self, block):
        for name in self.ENGS:
            e = self.eng[name]
            if not e.ops:
                continue
            if e.lazy and e.pending:
                raise RuntimeError("last op on lazy engine must be signaled")

            def body(h, e=e):
                for o in e.ops:
                    if o[0] == "wait":
                        h.wait_ge(o[1], o[2])
                    elif o[0] == "op":
                        ins = o[1](h)
                        if o[2] is not None:
                            ins.then_inc(o[2], 1)
                    elif o[0] == "cc":
                        o[1](h).then_inc(o[2])
                    else:
                        o[1](h).then_inc(o[2], 16)
            getattr(block, name)(body)


class Ring:
    def __init__(self, tiles):
        self.tiles = tiles
        self.i = 0

    def next(self):
        t = self.tiles[self.i % len(self.tiles)]
        self.i += 1
        return t


def rel_bucket_np(dist):
    n = np.maximum(dist, 0)
    max_exact = N_BUCKETS // 2
    nf = np.maximum(n, 1).astype(np.float32)
    large = max_exact + (np.log(nf / np.float32(max_exact)) / np.float32(math.log(MAX_DISTANCE / max_exact))
                         * np.float32(N_BUCKETS - max_exact)).astype(np.int32)
    large = np.minimum(large, N_BUCKETS - 1)
    return np.where(n < max_exact, n, large)


def build_program(S, depth=DEPTH, phases="swmABC", dbg=False):
    NT = S // TT
    NKT = S // 128
    NBLK = S // 256
    nc = bass.Bass("TRN2", target_bir_lowering=False)
    st = ExitStack()

    def din(name, shape, dt=F32):
        return nc.dram_tensor(name, shape, dt, kind="ExternalInput").ap()

    xT = din("xT", [KC, 128, S])
    cTs = din("cTs", [128, 4, 4])
    bsel = din("bsel", [4, 2])
    w_ada_s = din("w_ada_s", [depth, 512, 6 * D]) if "m" in phases else None
    b_ada_t = din("b_ada_t", [depth, 128, 192])
    wsh = {
        "in": din("w_in_s", [depth, D // NCORES, D_PROJ]),
        "o": din("w_o_s", [depth, D // NCORES, D]),
        "gate": din("w_gate_s", [depth, D // NCORES, D_FF]),
        "up": din("w_up_s", [depth, D // NCORES, D_FF]),
        "down": din("w_down_s", [depth, D_FF // NCORES, D]),
    } if "w" in phases else None
    wshape = {"in": (D, D_PROJ), "o": (D, D), "gate": (D, D_FF), "up": (D, D_FF), "down": (D_FF, D)}
    lng_t = din("lng_t", [depth, 2, 128, KC])
    lnb_t = din("lnb_t", [depth, 2, 128, KC])
    sinks_rep = din("sinks_rep", [depth, 128, H_B])
    lam_rep = din("lam_rep", [depth, 128, 256])
    subg_t = din("subg_t", [128, depth])
    strips = din("strips", [32, 128, 12 * 128])
    ownhot = din("ownhot", [128, NT * 4, 16])
    c31_rep = din("c31_rep", [128, 32])
    pastmask = din("pastmask", [128, NT * 4, 16])
    e_all = din("e_all", [16, 16 * 128], BF16)
    ident_in = din("ident", [128, 128])
    outT = nc.dram_tensor("outT", [KC, 128, S], F32, kind="ExternalOutput").ap()

    wsrc, wfull = {}, {}
    for l in range(depth):
        for k, (r, c) in wshape.items():
            wsrc[(k, l)] = nc.dram_tensor("wsrc_%s%d" % (k, l), [r // NCORES, c], BF16).ap()
            wfull[(k, l)] = nc.dram_tensor("wfull_%s%d" % (k, l), [r, c], BF16).ap()
    modp = nc.dram_tensor("modp", [4, depth * 6 * D], F32).ap()
    modr = nc.dram_tensor("modr", [4, depth * 6 * D], F32).ap()
    QKT = nc.dram_tensor("QKT", [68, 128, S], BF16).ap()
    VTOK = nc.dram_tensor("VTOK", [S, V_COLS], BF16).ap()
    KTL = nc.dram_tensor("KTL", [18 * 128, S], BF16).ap()
    KTG = nc.dram_tensor("KTG", [2 * 18 * 128, S], BF16).ap()
    VG = nc.dram_tensor("VG", [2 * S, V_COLS], BF16).ap()
    OT = nc.dram_tensor("OT", [KC, 128, S], BF16).ap()
    X1T = nc.dram_tensor("X1T", [KC, 128, S], F32).ap()

    P = Prog(nc, st)
    RG = [list(range(NCORES))]
    RGP = [[2 * i, 2 * i + 1] for i in range(NCORES // 2)]
    B_KTL, B_KTG, B_VG = Buf("KTL"), Buf("KTG"), Buf("VG")

    def kidx_of(ch):
        if CH_KA <= ch < CH_KA + 8:
            return ch - CH_KA
        if CH_KB <= ch < CH_KB + 2:
            return 8 + ch - CH_KB
        if CH_KC <= ch < CH_KC + 8:
            return 10 + ch - CH_KC
        return None

    B_wsrc = {k: Buf("wsrc_%s%d" % k) for k in wsrc}
    B_wfull = {k: Buf("wfull_%s%d" % k) for k in wfull}
    B_modp, B_modr = Buf("modp"), Buf("modr")
    B_QKT, B_VTOK, B_OT, B_X1T, B_out = Buf("QKT"), Buf("VTOK"), Buf("OT"), Buf("X1T"), Buf("outT")

    uid = [0]

    def sb(stack, name, shape, dt):
        uid[0] += 1
        return stack.enter_context(nc.sbuf_tensor("%s_u%d" % (name, uid[0]), shape, dt)), Buf(name)

    def sbc(stack, name, shape, dt, n):
        uid[0] += 1
        return stack.enter_context(nc.sbuf_tensor("%s_u%d" % (name, uid[0]), shape, dt)), [Buf("%s_c%d" % (name, i)) for i in range(n)]

    ps = []
    for i in range(8):
        ps.append((st.enter_context(nc.psum_tensor("ps%d" % i, [128, 512], F32)), Buf("ps%d" % i)))

    modv, B_modv = sb(st, "modv", [128, depth, 192], F32)
    lng, B_lng = sb(st, "lng", [128, depth * 2, KC], F32)
    lnb, B_lnb = sb(st, "lnb", [128, depth * 2, KC], F32)
    ones_f, B_ones_f = sb(st, "ones_f", [128, 128], F32)
    ones_b, B_ones_b = sb(st, "ones_b", [128, 128], BF16)
    ident, B_ident = sb(st, "ident", [128, 128], F32)
    c31, B_c31 = sb(st, "c31", [128, 32], F32)
    eall, B_eall = sb(st, "eall", [16, 16 * 128], BF16)
    sexp, B_sexp = sb(st, "sexp", [128, depth, H_B], F32)
    lamv, B_lamv = sb(st, "lamv", [128, depth, 4], F32)
    subg, B_subg = sb(st, "subg", [128, depth], F32)

    block = st.enter_context(nc.Block())

    rr = {"cast": 0, "evac": 0}

    def cast_op(out, in_, reads, writes, engines=("vector", "scalar")):
        e = engines[rr["cast"] % len(engines)]
        rr["cast"] += 1
        if e == "scalar":
            return P.op("scalar", lambda h: h.activation(out=out, in_=in_, func=AF.Copy), reads, writes)
        return P.op(e, lambda h: h.tensor_copy(out=out, in_=in_), reads, writes)

    def evac_op(out, in_, reads, writes):
        e = ("vector", "scalar")[rr["evac"] % 2]
        rr["evac"] += 1
        if e == "scalar":
            return P.op("scalar", lambda h: h.activation(out=out, in_=in_, func=AF.Copy), reads, writes)
        return P.op("vector", lambda h: h.tensor_copy(out=out, in_=in_), reads, writes)

    P.dma("sync", lambda h: h.dma_start(out=ident[:], in_=ident_in[:]), writes=[B_ident], sembuf=B_ident)
    P.dma("sync", lambda h: h.dma_start(out=c31[:], in_=c31_rep[:]), writes=[B_c31], sembuf=B_c31)
    P.dma("sync", lambda h: h.dma_start(out=eall[:], in_=e_all[:]), writes=[B_eall], sembuf=B_eall)
    P.dma("sync", lambda h: h.dma_start(out=subg[:], in_=subg_t[:]), writes=[B_subg], sembuf=B_subg)
    P.dma("sync", lambda h: h.dma_start(out=lng[:], in_=lng_t.rearrange("l t p c -> p (l t) c")), writes=[B_lng], sembuf=B_lng)
    P.dma("sync", lambda h: h.dma_start(out=lnb[:], in_=lnb_t.rearrange("l t p c -> p (l t) c")), writes=[B_lnb], sembuf=B_lnb)
    P.op("vector", lambda h: h.memset(ones_f[:], 1.0), writes=[B_ones_f])
    P.op("vector", lambda h: h.memset(ones_b[:], 1.0), writes=[B_ones_b])

    def phase_weights(l):
        with ExitStack() as ph:
            PIECE = 4096
            fr = Ring([sb(ph, "wc_f%d" % i, [128, PIECE], F32) for i in range(3)])
            br = Ring([sb(ph, "wc_b%d" % i, [128, PIECE], BF16) for i in range(3)])
            for k in ("in", "o", "gate", "up", "down"):
                r, c = wshape[k]
                per = (r // NCORES) * c // 128
                src_flat = wsh[k][l].rearrange("r c -> (r c)").rearrange("(p f) -> p f", p=128)
                dst_flat = wsrc[(k, l)].rearrange("r c -> (r c)").rearrange("(p f) -> p f", p=128)
                f0 = 0
                while f0 < per:
                    n = min(PIECE, per - f0)
                    (ft, fb), (bt, bb) = fr.next(), br.next()
                    P.dma("sync", lambda h, ft=ft, f0=f0, n=n, src_flat=src_flat: h.dma_start(out=ft[:, 0:n], in_=src_flat[:, f0:f0 + n]),
                          writes=[fb], sembuf=fb)
                    cast_op(bt[:, 0:n], ft[:, 0:n], [fb], [bb])
                    P.dma("sync", lambda h, bt=bt, f0=f0, n=n, dst_flat=dst_flat: h.dma_start(out=dst_flat[:, f0:f0 + n], in_=bt[:, 0:n]),
                          reads=[bb], writes=[B_wsrc[(k, l)]], sembuf=bb)
                    f0 += n
                P.cc(lambda h, k=k: h.collective_compute("AllGather", ALU.bypass, replica_groups=RG,
                                                         ins=[wsrc[(k, l)][:]], outs=[wfull[(k, l)][:]]),
                     reads=[B_wsrc[(k, l)]], writes=[B_wfull[(k, l)]], sembuf=B_wfull[(k, l)])
            P.barrier()

    def phase_mod():
        with ExitStack() as ph:
            cs_in, B_cs_in = sb(ph, "cs_in", [128, 4, 4], F32)
            cs, B_cs = sb(ph, "cs", [128, 4, 4], F32)
            bs, B_bs = sb(ph, "bs", [4, 2], F32)
            wr = Ring([sb(ph, "wada%d" % i, [128, 4, 2048], F32) for i in range(2)])
            sr = Ring([sb(ph, "mst%d" % i, [4, 2048], F32) for i in range(2)])
            m4r = Ring([sb(ph, "m4_%d" % i, [4, D], F32) for i in range(2)])
            bad, B_bad = sb(ph, "bad", [128, depth, 192], F32)
            P.dma("sync", lambda h: h.dma_start(out=cs_in[:], in_=cTs[:]), writes=[B_cs_in], sembuf=B_cs_in)
            P.dma("sync", lambda h: h.dma_start(out=bs[:], in_=bsel[:]), writes=[B_bs], sembuf=B_bs)
            P.dma("sync", lambda h: h.dma_start(out=bad[:], in_=b_ada_t.rearrange("l p c -> p l c")), writes=[B_bad], sembuf=B_bad)
            P.op("scalar", lambda h: h.activation(out=cs[:], in_=cs_in[:], func=AF.Silu), [B_cs_in], [B_cs])
            pi = 0
            for l in range(depth):
                wv = w_ada_s[l].rearrange("(c p) n -> p c n", p=128)
                for pc in range(6 * D // 2048):
                    wt, wb = wr.next()
                    P.dma("sync", lambda h, wt=wt, pc=pc, wv=wv: h.dma_start(out=wt[:], in_=wv[:, :, pc * 2048:(pc + 1) * 2048]),
                          writes=[wb], sembuf=wb)
                    stt, stb = sr.next()
                    for nb in range(4):
                        pt, pb = ps[pi % 8]
                        pi += 1
                        for kc in range(4):
                            P.op("tensor", lambda h, pt=pt, wt=wt, kc=kc, nb=nb: h.matmul(
                                pt[0:4, :], lhsT=cs[:, kc, :], rhs=wt[:, kc, nb * 512:(nb + 1) * 512], start=(kc == 0), stop=(kc == 3)),
                                reads=[B_cs, wb], writes=[pb], sig=(kc == 3))
                        evac_op(stt[:, nb * 512:(nb + 1) * 512], pt[0:4, :], [pb], [stb])
                    col = l * 6 * D + pc * 2048
                    P.dma("sync", lambda h, stt=stt, col=col: h.dma_start(out=modp[:, col:col + 2048], in_=stt[:]),
                          reads=[stb], writes=[B_modp], sembuf=stb)
            mp2 = modp.rearrange("b n -> (b n)").rearrange("(p f) -> p f", p=128)
            mr2 = modr.rearrange("b n -> (b n)").rearrange("(p f) -> p f", p=128)
            P.cc(lambda h: h.collective_compute("AllReduce", ALU.add, replica_groups=RG, ins=[mp2], outs=[mr2]),
                 reads=[B_modp], writes=[B_modr], sembuf=B_modr)
            for l in range(depth):
                for v in range(6):
                    mt, mb = m4r.next()
                    col = (l * 6 + v) * D
                    P.dma("sync", lambda h, mt=mt, col=col: h.dma_start(out=mt[:], in_=modr[:, col:col + D]),
                          reads=[B_modr], writes=[mb], sembuf=mb)
                    pt, pb = ps[pi % 8]
                    pi += 1
                    for c in range(KC):
                        P.op("tensor", lambda h, pt=pt, mt=mt, c=c: h.matmul(
                            pt[:, 2 * c:2 * c + 2], lhsT=mt[:, c * 128:(c + 1) * 128], rhs=bs[:, :], start=True, stop=True),
                            reads=[mb, B_bs], writes=[pb], sig=(c == KC - 1))
                    pv = pt[:, 0:2 * KC].rearrange("p (c two) -> p c two", two=2)[:, :, 0]
                    if v in (1, 4):
                        P.op("vector", lambda h, pv=pv, l=l, v=v: h.scalar_tensor_tensor(
                            out=modv[:, l, v * 32:(v + 1) * 32], in0=pv, scalar=1.0, in1=bad[:, l, v * 32:(v + 1) * 32],
                            op0=ALU.add, op1=ALU.add), [pb, B_bad], [B_modv])
                    else:
                        P.op("vector", lambda h, pv=pv, l=l, v=v: h.tensor_tensor(
                            out=modv[:, l, v * 32:(v + 1) * 32], in0=pv, in1=bad[:, l, v * 32:(v + 1) * 32], op=ALU.add),
                            [pb, B_bad], [B_modv])
            P.barrier()

    class WStream:
        def __init__(self, ring):
            self.ring = ring
            self.jobs = []
            self.issued = 0
            self.slots = []

        def add(self, parts):
            self.jobs.append(parts)
            return len(self.jobs) - 1

        def _issue(self, j):
            t, b = self.ring.next()
            for (key, k0, kn, c0, cn, dc) in self.jobs[j]:
                wv = wfull[key].rearrange("(c p) n -> p c n", p=128)
                P.dma("sync", lambda h, t=t, wv=wv, k0=k0, kn=kn, c0=c0, cn=cn, dc=dc: h.dma_start(
                    out=t[:, 0:kn, dc:dc + cn], in_=wv[:, k0:k0 + kn, c0:c0 + cn]),
                    reads=[B_wfull[key]], writes=[b], sembuf=b)
            self.slots.append((t, b))

        def get(self, j):
            la = len(self.ring.tiles) - 1
            while self.issued < min(len(self.jobs), j + la + 1):
                self._issue(self.issued)
                self.issued += 1
            return self.slots[j]

    def phase_A(l, xsrc, B_xsrc):
        with ExitStack() as ph:
            hT, B_hT = sbc(ph, "hT", [128, KC, TT], BF16, KC)
            xr = Ring([sb(ph, "xs%d" % i, [128, 4, TT], F32) for i in range(3)])
            wring = Ring([sb(ph, "wA%d" % i, [128, 8, 512], BF16) for i in range(4)])
            ostr = Ring([sb(ph, "ostA%d" % i, [128, 512], BF16) for i in range(4)])
            sc1p = modv[:, l, 32:64]
            sh1 = modv[:, l, 0:32]
            fm_blocks = []
            for lo, hi in ((0, 16), (24, 42), (44, 60)):
                c = lo
                while c < hi:
                    n = min(4, hi - c)
                    fm_blocks.append((c, n))
                    c += n
            tm_blocks = [(CH_VA * 128, 512, VOFF_A), (CH_VA * 128 + 512, 512, VOFF_A + 512), (CH_VB * 128, 256, VOFF_B),
                         (CH_VC * 128, 512, VOFF_C), (CH_VC * 128 + 512, 512, VOFF_C + 512)]
            for t in range(NT):
                t0 = t * TT
                for g in range(KC // 4):
                    xt, xb = xr.next()
                    P.dma("sync", lambda h, xt=xt, g=g, t0=t0: h.dma_start(
                        out=xt[:], in_=xsrc[g * 4:(g + 1) * 4, :, t0:t0 + TT].rearrange("c p s -> p c s")),
                        reads=[B_xsrc], writes=[xb], sembuf=xb)
                    for j in range(4):
                        kc = g * 4 + j
                        if kc % 2 == 0:
                            P.op("scalar", lambda h, xt=xt, j=j, kc=kc: h.activation(
                                out=hT[:, kc, :], in_=xt[:, j, :], func=AF.Identity, scale=sc1p[:, kc:kc + 1], bias=sh1[:, kc:kc + 1]),
                                [xb, B_modv], [B_hT[kc]])
                        else:
                            P.op("vector", lambda h, xt=xt, j=j, kc=kc: h.tensor_scalar(
                                out=hT[:, kc, :], in0=xt[:, j, :], scalar1=sc1p[:, kc:kc + 1], scalar2=sh1[:, kc:kc + 1],
                                op0=ALU.mult, op1=ALU.add), [xb, B_modv], [B_hT[kc]])
                ws = WStream(wring)
                plan = []
                for (c, n) in fm_blocks:
                    plan.append(("fm", c, n, [ws.add([(("in", l), kg * 8, 8, c * 128, n * 128, 0)]) for kg in range(4)]))
                for (c0, cn, vo) in tm_blocks:
                    plan.append(("tm", c0, cn, vo, [ws.add([(("in", l), kg * 8, 8, c0, cn, 0)]) for kg in range(4)]))
                bank = 0
                for item in plan:
                    if item[0] == "fm":
                        _, c, n, jobs = item
                        banks = [ps[(bank + i) % 8] for i in range(n)]
                        bank += 4
                        for kg, j in enumerate(jobs):
                            wt, wb = ws.get(j)
                            for k in range(8):
                                kk = kg * 8 + k
                                for i in range(n):
                                    pt, pb = banks[i]
                                    P.op("tensor", lambda h, pt=pt, wt=wt, k=k, i=i, kk=kk: h.matmul(
                                        pt[:], lhsT=wt[:, k, i * 128:(i + 1) * 128], rhs=hT[:, kk, :], start=(kk == 0), stop=(kk == KC - 1)),
                                        reads=[wb, B_hT[kk]], writes=[pb], sig=(kk == KC - 1) or (k == 7 and i == n - 1))
                        for i in range(n):
                            pt, pb = banks[i]
                            ot, ob = ostr.next()
                            evac_op(ot[:], pt[:], [pb], [ob])
                            ki = kidx_of(c + i)
                            if ki is None:
                                P.dma("gpsimd", lambda h, ot=ot, ch=c + i, t0=t0: h.dma_start(out=QKT[ch, :, t0:t0 + TT], in_=ot[:]),
                                      reads=[ob], writes=[B_QKT], sembuf=ob)
                            else:
                                P.dma("gpsimd", lambda h, ot=ot, ki=ki, t0=t0: h.dma_start(
                                    out=KTL[ki * 128:(ki + 1) * 128, t0:t0 + TT], in_=ot[:]),
                                    reads=[ob], writes=[B_KTL], sembuf=ob)
                    else:
                        _, c0, cn, vo, jobs = item
                        banks = [ps[(bank + i) % 8] for i in range(4)]
                        bank += 4
                        for kg, j in enumerate(jobs):
                            wt, wb = ws.get(j)
                            for k in range(8):
                                kk = kg * 8 + k
                                for ts in range(4):
                                    pt, pb = banks[ts]
                                    P.op("tensor", lambda h, pt=pt, wt=wt, k=k, ts=ts, kk=kk, cn=cn: h.matmul(
                                        pt[:, 0:cn], lhsT=hT[:, kk, ts * 128:(ts + 1) * 128], rhs=wt[:, k, 0:cn],
                                        start=(kk == 0), stop=(kk == KC - 1)),
                                        reads=[wb, B_hT[kk]], writes=[pb], sig=(kk == KC - 1) or (k == 7 and ts == 3))
                        for ts in range(4):
                            pt, pb = banks[ts]
                            ot, ob = ostr.next()
                            evac_op(ot[:, 0:cn], pt[:, 0:cn], [pb], [ob])
                            P.dma("gpsimd", lambda h, ot=ot, r0=t0 + ts * 128, vo=vo, cn=cn: h.dma_start(
                                out=VTOK[r0:r0 + 128, vo:vo + cn], in_=ot[:, 0:cn]),
                                reads=[ob], writes=[B_VTOK], sembuf=ob)
            for ki in range(18):
                P.cc(lambda h, ki=ki: h.collective_compute("AllGather", ALU.bypass, replica_groups=RGP,
                                                           ins=[KTL[ki * 128:(ki + 1) * 128, :]], outs=[KTG[ki * 256:(ki + 1) * 256, :]]),
                     reads=[B_KTL], writes=[B_KTG], sembuf=B_KTG)
            for tt in range(S // 128):
                P.cc(lambda h, tt=tt: h.collective_compute("AllGather", ALU.bypass, replica_groups=RGP,
                                                           ins=[VTOK[tt * 128:(tt + 1) * 128, :]], outs=[VG[tt * 256:(tt + 1) * 256, :]]),
                     reads=[B_VTOK], writes=[B_VG], sembuf=B_VG)
            P.barrier()

    def attn_core(ph_tiles, KT_ap, KT_buf, kbase, QT_ap, QT_buf, V_ap, V_buf, q0, nq, tiles, scale, banks_s, bank_o, bank_d,
                  cbias_ap, ptr, tmpr, mask=None):
        (po, pob), (pd, pdb) = bank_o, bank_d
        nt = len(tiles)
        for ti, tl in enumerate(tiles):
            pst, psb = banks_s[ti % len(banks_s)]
            qlo = tl["qlo"]
            has_mask = tl.get("mask") is not None
            P.op("tensor", lambda h, pst=pst, tl=tl, qlo=qlo, has_mask=has_mask: h.matmul(
                pst[:, qlo:nq], lhsT=KT_ap(tl["kt"]), rhs=QT_ap(qlo, nq), start=True, stop=not has_mask),
                reads=[KT_buf, QT_buf], writes=[psb], sig=not has_mask)
            if has_mask:
                mj, mlo, m_ap, m_buf = tl["mask"]
                P.op("tensor", lambda h, pst=pst, mj=mj, mlo=mlo, m_ap=m_ap: h.matmul(
                    pst[:, mlo:nq], lhsT=eall[:, mj * 128:(mj + 1) * 128], rhs=m_ap[:, mlo:nq], start=False, stop=True),
                    reads=[B_eall, m_buf], writes=[psb], sig=True)
            pt_, ptb = ptr.next()
            for (lo, hi, kind, bap, bbuf) in tl["bias"]:
                if kind == "const":
                    P.op("scalar", lambda h, pst=pst, pt_=pt_, lo=lo, hi=hi, bap=bap: h.activation(
                        out=pt_[:, lo:hi], in_=pst[:, lo:hi], func=AF.Exp, scale=scale, bias=bap),
                        [psb, bbuf], [ptb])
                else:
                    tt_, ttb = tmpr.next()
                    P.op("vector", lambda h, pst=pst, tt_=tt_, lo=lo, hi=hi, bap=bap: h.scalar_tensor_tensor(
                        out=tt_[:, lo:hi], in0=pst[:, lo:hi], scalar=scale, in1=bap, op0=ALU.mult, op1=ALU.add),
                        [psb, bbuf], [ttb])
                    P.op("scalar", lambda h, tt_=tt_, pt_=pt_, lo=lo, hi=hi: h.activation(
                        out=pt_[:, lo:hi], in_=tt_[:, lo:hi], func=AF.Exp), [ttb], [ptb])
            first, last = (ti == 0), (ti == nt - 1)
            P.op("tensor", lambda h, po=po, pt_=pt_, tl=tl, qlo=qlo, first=first, last=last: h.matmul(
                po[:, qlo:nq], lhsT=V_ap(tl["kt"]), rhs=pt_[:, qlo:nq], start=first, stop=last),
                reads=[V_buf, ptb], writes=[pob], sig=False)
            P.op("tensor", lambda h, pd=pd, pt_=pt_, qlo=qlo, first=first, last=last: h.matmul(
                pd[:, qlo:nq], lhsT=ones_b[:], rhs=pt_[:, qlo:nq], start=first, stop=last),
                reads=[B_ones_b, ptb], writes=[pdb], sig=True)

    def phase_B(l):
        with ExitStack() as ph:
            NQB = S // 512
            SG = 2 * S
            NKTG = SG // 128
            NBLKG = SG // 256
            qtr = Ring([sb(ph, "QTh%d" % i, [128, S], BF16) for i in range(2)])
            ktr = Ring([sb(ph, "KTh%d" % i, [128, SG], BF16) for i in range(2)])
            vr = Ring([sb(ph, "Vh%d" % i, [128, NKTG, 128], BF16) for i in range(2)])
            ptr = Ring([sb(ph, "PT%d" % i, [128, 512], BF16) for i in range(4)])
            tmpr = Ring([sb(ph, "TMP%d" % i, [128, 512], F32) for i in range(3)])
            bdr = Ring([sb(ph, "BD%d" % i, [128, 12 * 128], F32) for i in range(2)])
            ostr = Ring([sb(ph, "ostB%d" % i, [128, 512], BF16) for i in range(3)])
            rdr = Ring([sb(ph, "RD%d" % i, [128, 512], F32) for i in range(3)])
            qtf, B_qtf = sb(ph, "QTf", [128, S], F32)
            kmean, B_kmean = sb(ph, "kmean", [128, NBLKG], F32)
            pmask, B_pmask = sb(ph, "pmask", [128, NT * 4, 16], F32)
            ohot, B_ohot = sb(ph, "ohot", [128, NT * 4, 16], F32)
            gmr = Ring([sb(ph, "GM%d" % i, [128, 16], F32) for i in range(2)])
            t8r = Ring([sb(ph, "T8%d" % i, [128, 8], F32) for i in range(2)])
            mbr = Ring([sb(ph, "MB%d" % i, [128, 16], F32) for i in range(2)])
            mtr = Ring([sb(ph, "MT%d" % i, [16, 512], BF16) for i in range(2)])
            P.dma("sync", lambda h: h.dma_start(out=pmask[:], in_=pastmask[:]), writes=[B_pmask], sembuf=B_pmask)
            P.dma("sync", lambda h: h.dma_start(out=ohot[:], in_=ownhot[:]), writes=[B_ohot], sembuf=B_ohot)

            def load_kv(kidx, voff):
                kt, kb = ktr.next()
                vt, vb = vr.next()
                for G in range(2 * NQB):
                    rk, lb = G % 2, G // 2
                    row = (kidx * 2 + rk) * 128
                    P.dma("sync", lambda h, G=G, row=row, lb=lb: h.dma_start(
                        out=kt[:, G * 512:(G + 1) * 512], in_=KTG[row:row + 128, lb * 512:(lb + 1) * 512]),
                        reads=[B_KTG], writes=[kb], sembuf=kb)
                    r0 = lb * 8 * 128
                    P.dma("sync", lambda h, G=G, r0=r0, rk=rk: h.dma_start(
                        out=vt[:, 4 * G:4 * G + 4, :],
                        in_=VG[r0:r0 + 1024, voff:voff + 128].rearrange("(j two p) d -> p j two d", two=2, p=128)[:, :, rk, :]),
                        reads=[B_VG], writes=[vb], sembuf=vb)
                return (kt, kb), (vt, vb)

            def load_head(qch, kidx, voff):
                qt, qb = qtr.next()
                P.dma("sync", lambda h: h.dma_start(out=qt[:], in_=QKT[qch]), reads=[B_QKT], writes=[qb], sembuf=qb)
                kk, vv = load_kv(kidx, voff)
                return (qt, qb), kk, vv

            def load_bias(hglob):
                bt, bb = bdr.next()
                P.dma("sync", lambda h: h.dma_start(out=bt[:], in_=strips[hglob]), writes=[bb], sembuf=bb)
                return bt, bb

            def tiles_for_block(i, bt, bb, cb_ap, mask_ap=None, mask_buf=None):
                tl = []
                for J in range(0, max(0, 8 * i - 1)):
                    d = {"kt": J, "qlo": 0, "bias": [(0, 512, "const", cb_ap, B_c31)]}
                    if mask_ap is not None:
                        d["mask"] = (J // 2, 0, mask_ap, mask_buf)
                    tl.append(d)
                for u in range(9):
                    J = 8 * i - 1 + u
                    if J < 0:
                        continue
                    d = {"kt": J, "qlo": 0, "bias": [(0, 512, "matrix", bt[:, (8 - u) * 128:(12 - u) * 128], bb)]}
                    if mask_ap is not None:
                        d["mask"] = (J // 2, 0, mask_ap, mask_buf)
                    tl.append(d)
                return tl

            sc128 = 128.0 ** -0.5
            for hh in range(H_A):
                (qt, qb), (kt, kb), (vt, vb) = load_head(CH_QA + hh, hh, VOFF_A + hh * 128)
                bt, bb = load_bias(hh)
                P.op("scalar", lambda h, qt=qt: h.activation(out=qtf[:], in_=qt[:], func=AF.Copy), [qb], [B_qtf])
                P.op("vector", lambda h, kt=kt: h.tensor_reduce(
                    out=kmean[:], in_=kt[:].rearrange("p (j k) -> p j k", k=256), axis=AX.X, op=ALU.add), [kb], [B_kmean])
                P.op("vector", lambda h: h.tensor_scalar(out=kmean[:], in0=kmean[:], scalar1=1.0 / 256.0, scalar2=None, op0=ALU.mult),
                     [B_kmean], [B_kmean])
                for g in range(NQB):
                    q0 = g * 512
                    mt, mtb = mtr.next()
                    for s in range(4):
                        pg, pgb = ps[4 + (s % 2)]
                        P.op("tensor", lambda h, pg=pg, qq=q0 + s * 128: h.matmul(
                            pg[:, 0:NBLKG], lhsT=qtf[:, qq:qq + 128], rhs=kmean[:, :], start=True, stop=True),
                            reads=[B_qtf, B_kmean], writes=[pgb], sig=True)
                        gm, gmb = gmr.next()
                        t8, t8b = t8r.next()
                        mb_, mbb = mbr.next()
                        if NBLKG < 16:
                            P.op("vector", lambda h, gm=gm: h.memset(gm[:], NEG), [], [gmb])
                        P.op("vector", lambda h, gm=gm, pg=pg, idx=g * 4 + s: h.tensor_tensor(
                            out=gm[:, 0:NBLKG], in0=pg[:, 0:NBLKG], in1=pmask[:, idx, 0:NBLKG], op=ALU.add), [pgb, B_pmask], [gmb])
                        P.op("vector", lambda h, gm=gm, t8=t8: h.max(out=t8[:], in_=gm[:]), [gmb], [t8b])
                        P.op("vector", lambda h, gm=gm, t8=t8, mb_=mb_: h.tensor_scalar(
                            out=mb_[:], in0=gm[:], scalar1=t8[:, 2:3], scalar2=MNEG, op0=ALU.is_ge, op1=ALU.mult), [gmb, t8b], [mbb])
                        P.op("vector", lambda h, mb_=mb_, idx=g * 4 + s: h.tensor_tensor(
                            out=mb_[:], in0=mb_[:], in1=ohot[:, idx, :], op=ALU.max), [mbb, B_ohot], [mbb])
                        ptp, ptpb = ps[6 + (s % 2)]
                        P.op("tensor", lambda h, ptp=ptp, mb_=mb_: h.transpose(out=ptp[0:16, 0:128], in_=mb_[:], identity=ident[:]),
                             reads=[mbb, B_ident], writes=[ptpb], sig=True)
                        P.op("vector", lambda h, mt=mt, ptp=ptp, s=s: h.tensor_scalar(
                            out=mt[:, s * 128:(s + 1) * 128], in0=ptp[0:16, 0:128], scalar1=-MNEG, scalar2=None, op0=ALU.add),
                            [ptpb], [mtb])
                    tiles = tiles_for_block(g, bt, bb, c31[:, hh:hh + 1], mt, mtb)
                    attn_core(None, lambda j, kt=kt: kt[:, j * 128:(j + 1) * 128], kb, 0,
                              lambda lo, hi, qt=qt, q0=q0: qt[:, q0 + lo:q0 + hi], qb,
                              lambda j, vt=vt: vt[:, j, :], vb, q0, 512, tiles, sc128,
                              [ps[0], ps[1]], ps[2], ps[3], None, ptr, tmpr)
                    rd, rdb = rdr.next()
                    ot, ob = ostr.next()
                    P.op("vector", lambda h, rd=rd: h.reciprocal(out=rd[:], in_=ps[3][0][:]), [ps[3][1]], [rdb])
                    P.op("vector", lambda h, rd=rd, ot=ot: h.tensor_tensor(out=ot[:], in0=ps[2][0][:], in1=rd[:], op=ALU.mult),
                         [ps[2][1], rdb], [ob])
                    P.dma("gpsimd", lambda h, ot=ot, ch=hh, q0=q0: h.dma_start(out=OT[ch, :, q0:q0 + 512], in_=ot[:]),
                          reads=[ob], writes=[B_OT], sembuf=ob)

            sc64 = 64.0 ** -0.5
            o0r = Ring([sb(ph, "O0_%d" % i, [128, 512], F32) for i in range(2)])
            for hh in range(H_C):
                (qt, qb), (kt, kb), (vt, vb) = load_head(CH_QC + hh, 10 + hh, VOFF_C + hh * 128)
                bt, bb = load_bias(H_A + H_B + hh)
                cb = c31[:, H_A + H_B + hh:H_A + H_B + hh + 1]
                for g in range(NQB):
                    q0 = g * 512
                    tiles = tiles_for_block(g, bt, bb, cb)
                    res = []
                    for m in range(2):
                        plo, phi = m * 64, (m + 1) * 64
                        attn_core(None, lambda j, kt=kt, plo=plo, phi=phi: kt[plo:phi, j * 128:(j + 1) * 128], kb, 0,
                                  lambda lo, hi, qt=qt, q0=q0, plo=plo, phi=phi: qt[plo:phi, q0 + lo:q0 + hi], qb,
                                  lambda j, vt=vt: vt[:, j, :], vb, q0, 512, tiles, sc64,
                                  [ps[0], ps[1]], ps[2 + 2 * m], ps[3 + 2 * m], None, ptr, tmpr)
                        rd, rdb = rdr.next()
                        om, omb = o0r.next()
                        P.op("vector", lambda h, rd=rd, m=m: h.reciprocal(out=rd[:], in_=ps[3 + 2 * m][0][:]), [ps[3 + 2 * m][1]], [rdb])
                        P.op("vector", lambda h, rd=rd, om=om, m=m: h.tensor_tensor(
                            out=om[:], in0=ps[2 + 2 * m][0][:], in1=rd[:], op=ALU.mult), [ps[2 + 2 * m][1], rdb], [omb])
                        res.append((om, omb))
                    (o0, o0b), (o1, o1b) = res
                    P.op("vector", lambda h, o0=o0, o1=o1: h.scalar_tensor_tensor(
                        out=o0[:], in0=o1[:], scalar=lamv[:, l, 1:2], in1=o0[:], op0=ALU.mult, op1=ALU.add),
                        [o0b, o1b, B_lamv], [o0b])
                    P.op("scalar", lambda h, o0=o0, o1=o1: h.activation(out=o1[:], in_=o0[:], func=AF.Square), [o0b], [o1b])
                    pss, pssb = ps[6]
                    P.op("tensor", lambda h, pss=pss, o1=o1: h.matmul(pss[:], lhsT=ones_f[:], rhs=o1[:], start=True, stop=True),
                         reads=[B_ones_f, o1b], writes=[pssb], sig=True)
                    rd, rdb = rdr.next()
                    P.op("scalar", lambda h, rd=rd, pss=pss: h.activation(out=rd[:], in_=pss[:], func=AF.Sqrt, scale=1.0 / 128.0, bias=epsv[:, 0:1]),
                         [pssb, B_epsv], [rdb])
                    P.op("vector", lambda h, rd=rd: h.reciprocal(out=rd[:], in_=rd[:]), [rdb], [rdb])
                    ot, ob = ostr.next()
                    P.op("vector", lambda h, o0=o0, rd=rd, ot=ot: h.scalar_tensor_tensor(
                        out=ot[:], in0=o0[:], scalar=lamv[:, l, 2:3], in1=rd[:], op0=ALU.mult, op1=ALU.mult),
                        [o0b, rdb, B_lamv], [ob])
                    P.dma("gpsimd", lambda h, ot=ot, ch=H_A + H_B + hh, q0=q0: h.dma_start(out=OT[ch, :, q0:q0 + 512], in_=ot[:]),
                          reads=[ob], writes=[B_OT], sembuf=ob)

            SW_IDX = (3, 4, 7, 8)
            bsw, B_bsw = sb(ph, "bsw", [128, 4, H_B, 128], F32)
            for a, idx in enumerate(SW_IDX):
                P.dma("sync", lambda h, a=a, idx=idx: h.dma_start(
                    out=bsw[:, a, :, :], in_=strips[H_A:H_A + H_B, :, idx * 128:(idx + 1) * 128].rearrange("h k q -> k h q")),
                    writes=[B_bsw], sembuf=B_bsw)
            sinke, B_sinke = sb(ph, "sinke", [128, H_B, 128], F32)
            for hq in range(H_B):
                P.op("vector", lambda h, hq=hq: h.tensor_scalar(
                    out=sinke[:, hq, :], in0=ones_f[:], scalar1=sexp[:, l, hq:hq + 1], scalar2=None, op0=ALU.mult),
                    [B_ones_f, B_sexp], [B_sinke])
            qbr = Ring([sb(ph, "QB%d" % i, [128, 8, 512], BF16) for i in range(2)])
            ostw = Ring([sb(ph, "ostW%d" % i, [128, 4, 128], BF16) for i in range(3)])
            for kv in range(KV_B):
                (kt, kb), (vt, vb) = load_kv(8 + kv, VOFF_B + kv * 128)
                for g in range(NQB):
                    q0 = g * 512
                    qt, qb = qbr.next()
                    P.dma("sync", lambda h, qt=qt, kv=kv, q0=q0: h.dma_start(
                        out=qt[:], in_=QKT[CH_QB + kv * 8:CH_QB + kv * 8 + 8, :, q0:q0 + 512].rearrange("c p s -> p c s")),
                        reads=[B_QKT], writes=[qb], sembuf=qb)
                    for s in range(4):
                        cands = []
                        for u in (s, s + 1, s + 4, s + 5):
                            J = 8 * g - 1 + u
                            if J >= 0:
                                cands.append((J, SW_IDX.index(8 - u + s)))
                        for hg in range(2):
                            h0 = hg * 4
                            hb0 = kv * 8 + h0
                            po, pob = ps[2 + hg]
                            pd, pdb = ps[4 + hg]
                            for ki, (kj, a) in enumerate(cands):
                                pst, psb = ps[ki % 2]
                                P.op("tensor", lambda h, pst=pst, kt=kt, kj=kj, qt=qt, h0=h0, s=s: h.matmul(
                                    pst[:].rearrange("p (a q) -> p a q", a=4), lhsT=kt[:, kj * 128:(kj + 1) * 128],
                                    rhs=qt[:, h0:h0 + 4, s * 128:(s + 1) * 128], start=True, stop=True),
                                    reads=[kb, qb], writes=[psb], sig=True)
                                tt_, ttb = tmpr.next()
                                pt_, ptb = ptr.next()
                                P.op("vector", lambda h, pst=pst, tt_=tt_, a=a, hb0=hb0: h.scalar_tensor_tensor(
                                    out=tt_[:].rearrange("p (a q) -> p a q", a=4), in0=pst[:].rearrange("p (a q) -> p a q", a=4),
                                    scalar=sc128, in1=bsw[:, a, hb0:hb0 + 4, :], op0=ALU.mult, op1=ALU.add),
                                    [psb, B_bsw], [ttb])
                                P.op("scalar", lambda h, tt_=tt_, pt_=pt_: h.activation(out=pt_[:], in_=tt_[:], func=AF.Exp), [ttb], [ptb])
                                first, last = (ki == 0), (ki == len(cands) - 1)
                                P.op("tensor", lambda h, po=po, vt=vt, kj=kj, pt_=pt_, first=first, last=last: h.matmul(
                                    po[:], lhsT=vt[:, kj, :], rhs=pt_[:], start=first, stop=last),
                                    reads=[vb, ptb], writes=[pob], sig=False)
                                P.op("tensor", lambda h, pd=pd, pt_=pt_, first=first, last=last: h.matmul(
                                    pd[:], lhsT=ones_b[:], rhs=pt_[:], start=first, stop=last),
                                    reads=[B_ones_b, ptb], writes=[pdb], sig=True)
                            rd, rdb = rdr.next()
                            P.op("vector", lambda h, rd=rd, pd=pd, hb0=hb0: h.tensor_tensor(
                                out=rd[:].rearrange("p (a q) -> p a q", a=4), in0=pd[:].rearrange("p (a q) -> p a q", a=4),
                                in1=sinke[:, hb0:hb0 + 4, :], op=ALU.add), [pdb, B_sinke], [rdb])
                            P.op("vector", lambda h, rd=rd: h.reciprocal(out=rd[:], in_=rd[:]), [rdb], [rdb])
                            ot, ob = ostw.next()
                            P.op("vector", lambda h, ot=ot, po=po, rd=rd: h.tensor_tensor(
                                out=ot[:], in0=po[:].rearrange("p (a q) -> p a q", a=4), in1=rd[:].rearrange("p (a q) -> p a q", a=4),
                                op=ALU.mult), [pob, rdb], [ob])
                            ch0 = H_A + hb0
                            P.dma("gpsimd", lambda h, ot=ot, ch0=ch0, qq=q0 + s * 128: h.dma_start(
                                out=OT[ch0:ch0 + 4, :, qq:qq + 128].rearrange("c p s -> p c s"), in_=ot[:]),
                                reads=[ob], writes=[B_OT], sembuf=ob)
            P.barrier()

    def layer_norm_tile(r, B_r, sqr, stat, l, which, out_fn):
        (mean, B_mean), (rstd, B_rstd), (msq, B_msq) = stat
        p1, p1b = ps[6]
        p2, p2b = ps[7]
        for n in range(KC):
            sq, sqb = sqr.next()
            P.op("scalar", lambda h, sq=sq, n=n: h.activation(out=sq[:], in_=r[:, n, :], func=AF.Square), [B_r[n]], [sqb])
            P.op("tensor", lambda h, n=n: h.matmul(p1[:], lhsT=ones_f[:], rhs=r[:, n, :], start=(n == 0), stop=(n == KC - 1)),
                 reads=[B_ones_f, B_r[n]], writes=[p1b], sig=False)
            P.op("tensor", lambda h, sq=sq, n=n: h.matmul(p2[:], lhsT=ones_f[:], rhs=sq[:], start=(n == 0), stop=(n == KC - 1)),
                 reads=[B_ones_f, sqb], writes=[p2b], sig=True)
        P.op("scalar", lambda h: h.activation(out=mean[:], in_=p1[:], func=AF.Copy, scale=1.0 / D), [p1b], [B_mean])
        P.op("vector", lambda h: h.tensor_tensor(out=msq[:], in0=mean[:], in1=mean[:], op=ALU.mult), [B_mean], [B_msq])
        P.op("vector", lambda h: h.scalar_tensor_tensor(out=msq[:], in0=p2[:], scalar=1.0 / D, in1=msq[:], op0=ALU.mult, op1=ALU.subtract),
             [p2b, B_msq], [B_msq])
        P.op("scalar", lambda h: h.activation(out=rstd[:], in_=msq[:], func=AF.Sqrt, bias=epsv[:, 0:1]), [B_msq, B_epsv], [B_rstd])
        P.op("vector", lambda h: h.reciprocal(out=rstd[:], in_=rstd[:]), [B_rstd], [B_rstd])
        gi = l * 2 + which
        for n in range(KC):
            P.op("vector", lambda h, n=n: h.tensor_tensor(out=r[:, n, :], in0=r[:, n, :], in1=mean[:], op=ALU.subtract), [B_r[n], B_mean], [B_r[n]])
            P.op("vector", lambda h, n=n: h.tensor_tensor(out=r[:, n, :], in0=r[:, n, :], in1=rstd[:], op=ALU.mult), [B_r[n], B_rstd], [B_r[n]])
            P.op("scalar", lambda h, n=n: h.activation(out=r[:, n, :], in_=r[:, n, :], func=AF.Identity,
                                                       scale=lng[:, gi, n:n + 1], bias=lnb[:, gi, n:n + 1]), [B_r[n], B_lng, B_lnb], [B_r[n]])
            out_fn(n)

    def phase_C(l, xsrc, B_xsrc, xdst, B_xdst):
        with ExitStack() as ph:
            r, B_r = sbc(ph, "rC", [128, KC, TT], F32, KC)
            a32, B_a32 = sbc(ph, "a32", [128, KC, TT], BF16, KC)
            act, B_act = sbc(ph, "actC", [128, 30, TT], BF16, 30)
            wring = Ring([sb(ph, "wC%d" % i, [128, 8, 512], BF16) for i in range(3)])
            xr = Ring([sb(ph, "xC%d" % i, [128, TT], F32) for i in range(3)])
            sqr = Ring([sb(ph, "sqC%d" % i, [128, TT], F32) for i in range(2)])
            sgr = Ring([sb(ph, "sgC%d" % i, [128, TT], F32) for i in range(2)])
            stat = [sb(ph, "st_mean", [128, TT], F32), sb(ph, "st_rstd", [128, TT], F32), sb(ph, "st_msq", [128, TT], F32)]
            ostr = Ring([sb(ph, "ostC%d" % i, [128, TT], F32) for i in range(2)])
            g1 = modv[:, l, 64:96]
            sh2 = modv[:, l, 96:128]
            sc2p = modv[:, l, 128:160]
            g2 = modv[:, l, 160:192]
            for t in range(NT):
                t0 = t * TT
                for c0 in range(0, KC, 8):
                    P.dma("sync", lambda h, t0=t0, c0=c0: h.dma_start(
                        out=a32[:, c0:c0 + 8, :], in_=OT[c0:c0 + 8, :, t0:t0 + TT].rearrange("c p s -> p c s")),
                        reads=[B_OT], writes=B_a32[c0:c0 + 8], sembuf=B_a32[c0])
                ws = WStream(wring)
                plan = []
                for nb in range(KC // 4):
                    plan.append([ws.add([(("o", l), kg * 8, 8, nb * 512, 512, 0)]) for kg in range(4)])
                bank = 0
                for nb, jobs in enumerate(plan):
                    banks = [ps[(bank + i) % 4] for i in range(4)] if False else [ps[(bank + i) % 8] for i in range(4)]
                    bank = (bank + 4) % 8
                    if bank == 0 and False:
                        pass
                    for kg, j in enumerate(jobs):
                        wt, wb = ws.get(j)
                        for k in range(8):
                            kk = kg * 8 + k
                            for i in range(4):
                                pt, pb = banks[i]
                                P.op("tensor", lambda h, pt=pt, wt=wt, k=k, i=i, kk=kk: h.matmul(
                                    pt[:], lhsT=wt[:, k, i * 128:(i + 1) * 128], rhs=a32[:, kk, :], start=(kk == 0), stop=(kk == KC - 1)),
                                    reads=[wb, B_a32[kk]], writes=[pb], sig=(kk == KC - 1) or (k == 7 and i == 3))
                    for i in range(4):
                        n = nb * 4 + i
                        pt, pb = banks[i]
                        xt, xb = xr.next()
                        P.dma("sync", lambda h, xt=xt, n=n, t0=t0: h.dma_start(out=xt[:], in_=xsrc[n, :, t0:t0 + TT]),
                              reads=[B_xsrc], writes=[xb], sembuf=xb)
                        P.op("scalar", lambda h, xt=xt: h.mul(xt[:], xt[:], ALPHA), [xb], [xb])
                        P.op("vector", lambda h, pt=pt, xt=xt, n=n: h.scalar_tensor_tensor(
                            out=r[:, n, :], in0=pt[:], scalar=g1[:, n:n + 1], in1=xt[:], op0=ALU.mult, op1=ALU.add),
                            [pb, xb, B_modv], [B_r[n]])

                def mk_h2(n):
                    P.op("vector", lambda h, n=n: h.tensor_scalar(
                        out=a32[:, n, :], in0=r[:, n, :], scalar1=sc2p[:, n:n + 1], scalar2=sh2[:, n:n + 1], op0=ALU.mult, op1=ALU.add),
                        [B_r[n], B_modv], [B_a32[n]])
                layer_norm_tile(r, B_r, sqr, stat, l, 0, mk_h2)
                for hi_, (f0, f1) in enumerate(FF_HALVES):
                    nf = f1 - f0
                    ws = WStream(wring)
                    gu_plan = []
                    for j2 in range(nf // 2):
                        c0 = (f0 + 2 * j2) * 128
                        gu_plan.append([ws.add([(("gate", l), kg * 8, 8, c0, 256, 0), (("up", l), kg * 8, 8, c0, 256, 256)])
                                        for kg in range(4)])
                    dn_plan = []
                    kgs = []
                    k = 0
                    while k < nf:
                        kgs.append((k, min(8, nf - k)))
                        k += 8
                    for nb in range(KC // 4):
                        dn_plan.append([ws.add([(("down", l), f0 + k0, kn, nb * 512, 512, 0)]) for (k0, kn) in kgs])
                    bank = 0
                    for j2, jobs in enumerate(gu_plan):
                        banks = [ps[(bank + i) % 8] for i in range(4)]
                        bank = (bank + 4) % 8
                        for kg, j in enumerate(jobs):
                            wt, wb = ws.get(j)
                            for k in range(8):
                                kk = kg * 8 + k
                                for i in range(4):
                                    pt, pb = banks[i]
                                    P.op("tensor", lambda h, pt=pt, wt=wt, k=k, i=i, kk=kk: h.matmul(
                                        pt[:], lhsT=wt[:, k, i * 128:(i + 1) * 128], rhs=a32[:, kk, :], start=(kk == 0), stop=(kk == KC - 1)),
                                        reads=[wb, B_a32[kk]], writes=[pb], sig=(kk == KC - 1) or (k == 7 and i == 3))
                        for i in range(2):
                            sg, sgb = sgr.next()
                            (pg, pgb), (pu, pub) = banks[i], banks[2 + i]
                            P.op("scalar", lambda h, sg=sg, pg=pg: h.activation(out=sg[:], in_=pg[:], func=AF.Silu), [pgb], [sgb])
                            P.op("vector", lambda h, sg=sg, pu=pu, jj=2 * j2 + i: h.tensor_tensor(
                                out=act[:, jj, :], in0=pu[:], in1=sg[:], op=ALU.mult), [pub, sgb], [B_act[2 * j2 + i]])
                    for nb, jobs in enumerate(dn_plan):
                        banks = [ps[(bank + i) % 8] for i in range(4)]
                        bank = (bank + 4) % 8
                        for (k0, kn), j in zip(kgs, jobs):
                            wt, wb = ws.get(j)
                            for k in range(kn):
                                kk = k0 + k
                                for i in range(4):
                                    pt, pb = banks[i]
                                    P.op("tensor", lambda h, pt=pt, wt=wt, k=k, i=i, kk=kk, nf=nf, kn=kn: h.matmul(
                                        pt[:], lhsT=wt[:, k, i * 128:(i + 1) * 128], rhs=act[:, kk, :], start=(kk == 0), stop=(kk == nf - 1)),
                                        reads=[wb, B_act[kk]], writes=[pb], sig=(kk == nf - 1) or (k == kn - 1 and i == 3))
                        for i in range(4):
                            n = nb * 4 + i
                            pt, pb = banks[i]
                            if hi_ == 0:
                                P.op("scalar", lambda h, n=n: h.mul(r[:, n, :], r[:, n, :], ALPHA), [B_r[n]], [B_r[n]])
                            P.op("vector", lambda h, pt=pt, n=n: h.scalar_tensor_tensor(
                                out=r[:, n, :], in0=pt[:], scalar=g2[:, n:n + 1], in1=r[:, n, :], op0=ALU.mult, op1=ALU.add),
                                [pb, B_r[n], B_modv], [B_r[n]])

                def mk_out(n, t0=t0):
                    ot, ob = ostr.next()
                    P.op("vector", lambda h, ot=ot, n=n: h.tensor_copy(out=ot[:], in_=r[:, n, :]), [B_r[n]], [ob])
                    P.dma("gpsimd", lambda h, ot=ot, n=n, t0=t0: h.dma_start(out=xdst[n, :, t0:t0 + TT], in_=ot[:]),
                          reads=[ob], writes=[B_xdst], sembuf=ob)
                layer_norm_tile(r, B_r, sqr, stat, l, 1, mk_out)
            P.barrier()

    epsv, B_epsv = sb(st, "epsv", [128, 1], F32)
    P.op("vector", lambda h: h.memset(epsv[:], LN_EPS), writes=[B_epsv])

    def phase_scalars():
        with ExitStack() as ph:
            lr, B_lr = sb(ph, "lr", [128, depth, 256], F32)
            sr_, B_sr = sexp, B_sexp
            tmp, B_tmp = sb(ph, "lt", [128, depth, 2, 64], F32)
            red, B_red = sb(ph, "lred", [128, depth, 2], F32)
            P.dma("sync", lambda h: h.dma_start(out=lr[:], in_=lam_rep.rearrange("l p k -> p l k")), writes=[B_lr], sembuf=B_lr)
            P.dma("sync", lambda h: h.dma_start(out=sr_[:], in_=sinks_rep.rearrange("l p k -> p l k")), writes=[B_sr], sembuf=B_sr)
            for l in range(depth):
                lam_init = 0.8 - 0.6 * math.exp(-0.3 * l)
                lv = lr[:, l, :].rearrange("p (a two k) -> p a two k", a=2, two=2)
                P.op("vector", lambda h, lv=lv, l=l: h.tensor_tensor(
                    out=tmp[:, l, :, :], in0=lv[:, :, 0, :], in1=lv[:, :, 1, :], op=ALU.mult), [B_lr], [B_tmp])
                P.op("vector", lambda h, l=l: h.tensor_reduce(out=red[:, l, :], in_=tmp[:, l, :, :], axis=AX.X, op=ALU.add), [B_tmp], [B_red])
                P.op("scalar", lambda h, l=l: h.activation(out=red[:, l, :], in_=red[:, l, :], func=AF.Exp), [B_red], [B_red])
                P.op("vector", lambda h, l=l, lam_init=lam_init: h.scalar_tensor_tensor(
                    out=lamv[:, l, 0:1], in0=red[:, l, 0:1], scalar=lam_init, in1=red[:, l, 1:2], op0=ALU.add, op1=ALU.subtract),
                    [B_red], [B_lamv])
                P.op("vector", lambda h, l=l: h.tensor_scalar(out=lamv[:, l, 1:2], in0=lamv[:, l, 0:1], scalar1=-1.0, scalar2=None, op0=ALU.mult),
                     [B_lamv], [B_lamv])
                P.op("vector", lambda h, l=l, lam_init=lam_init: h.tensor_scalar(
                    out=lamv[:, l, 2:3], in0=subg[:, l:l + 1], scalar1=1.0 - lam_init, scalar2=None, op0=ALU.mult), [B_subg], [B_lamv])
                P.op("scalar", lambda h, l=l: h.activation(out=sr_[:, l, :], in_=sr_[:, l, :], func=AF.Exp), [B_sr], [B_sr])
            P.barrier()

    if "s" in phases:
        phase_scalars()
    if "w" in phases:
        phase_weights(0)
    if "m" in phases:
        phase_mod()
    if depth > 1 and "w" in phases:
        phase_weights(1)
    for l in range(depth):
        xsrc, bx = (xT, Buf("xT")) if l == 0 else (X1T, B_X1T)
        xdst, bd = (outT, B_out) if l == depth - 1 else (X1T, B_X1T)
        if "A" in phases:
            phase_A(l, xsrc, bx)
        if "B" in phases:
            phase_B(l)
        if "C" in phases:
            phase_C(l, xsrc, bx, xdst, bd)
    if dbg:
        dbg_mod = nc.dram_tensor("dbg_mod", [128, depth * 192], F32, kind="ExternalOutput").ap()
        P.dma("sync", lambda h: h.dma_start(out=dbg_mod[:], in_=modv[:].rearrange("p l c -> p (l c)")), reads=[B_modv], sembuf=B_modv)
        dbg_lam = nc.dram_tensor("dbg_lam", [128, depth * 4], F32, kind="ExternalOutput").ap()
        P.dma("sync", lambda h: h.dma_start(out=dbg_lam[:], in_=lamv[:].rearrange("p l c -> p (l c)")), reads=[B_lamv], sembuf=B_lamv)
        for nm, t in (("QKT", QKT), ("VTOK", VTOK), ("OT", OT)):
            shp = list(t.shape)
            d = nc.dram_tensor("dbg_" + nm, shp, BF16, kind="ExternalOutput").ap()
            bb = {"QKT": B_QKT, "VTOK": B_VTOK, "OT": B_OT}[nm]
            if nm == "VTOK":
                for i in range(shp[0] // 128):
                    P.dma("sync", lambda h, d=d, t=t, i=i: h.dma_start(out=d[i * 128:(i + 1) * 128, :], in_=t[i * 128:(i + 1) * 128, :]),
                          reads=[bb], sembuf=Buf("dbgcp_" + nm))
            else:
                for i in range(shp[0]):
                    P.dma("sync", lambda h, d=d, t=t, i=i: h.dma_start(out=d[i], in_=t[i]), reads=[bb], sembuf=Buf("dbgcp_" + nm))
    print("n_ops", P.n_ops, "n_sems", len(P.sems))
    P.barrier()
    P.emit(block)
    st.close()
    return nc


def host_constants(S, rel_bias, par):
    k = np.arange(128)[:, None]
    q = np.arange(128)[None, :]
    d_diag = q - k
    d_prev = 128 + q - k
    bk_d = rel_bucket_np(d_diag)
    bk_p = rel_bucket_np(d_prev)
    negt = np.full((128, 128), NEG, np.float32)
    strips = np.empty((32, 128, 12, 128), np.float32)
    for h in range(32):
        swa = H_A <= h < H_A + H_B
        t_diag = np.where(d_diag >= 0, rel_bias[bk_d, h], np.float32(NEG)).astype(np.float32)
        t_prev = rel_bias[bk_p, h].astype(np.float32)
        if swa:
            t_prev = np.where(d_prev < 128, t_prev, np.float32(NEG)).astype(np.float32)
        t_far = negt if swa else np.full((128, 128), rel_bias[31, h], np.float32)
        for idx in range(12):
            rel = (idx - 8) + 1 + 4 * par
            strips[h, :, idx, :] = negt if rel < 0 else t_diag if rel == 0 else t_prev if rel == 1 else t_far
    c31 = np.ascontiguousarray(np.broadcast_to(rel_bias[31][None, :], (128, 32))).astype(np.float32)
    NT = S // TT
    pm = np.zeros((128, NT * 4, 16), np.float32)
    oh = np.zeros((128, NT * 4, 16), np.float32)
    for i in range(NT):
        for s in range(4):
            qblk = 4 * i + 2 * par + s // 2
            pm[:, i * 4 + s, qblk:] = NEG
            if qblk < 16:
                oh[:, i * 4 + s, qblk] = MNEG
    e = np.zeros((16, 16, 128), np.float32)
    for j in range(16):
        e[j, j, :] = 1.0
    import ml_dtypes
    e = e.reshape(16, 16 * 128).astype(ml_dtypes.bfloat16)
    return {"strips": strips.reshape(32, 128, 12 * 128), "c31_rep": c31, "pastmask": pm, "ownhot": oh, "e_all": e,
            "ident": np.eye(128, dtype=np.float32)}


def host_inputs(SL, x, c, rel_bias, w_ada, b_ada, w_in, w_o, attn_sinks, diff_lambda, diff_subln_g,
                ln_g, ln_b, w_gate, w_up, w_down, depth=DEPTH):
    DEPTH = depth
    (w_ada, b_ada, w_in, w_o, attn_sinks, diff_lambda, diff_subln_g, ln_g, ln_b, w_gate, w_up, w_down) = [
        a[:depth] for a in (w_ada, b_ada, w_in, w_o, attn_sinks, diff_lambda, diff_subln_g, ln_g, ln_b, w_gate, w_up, w_down)]
    consts = [host_constants(SL, rel_bias, par) for par in range(2)]
    shared = {
        "b_ada_t": np.ascontiguousarray(b_ada.reshape(DEPTH, 6 * KC, 128).transpose(0, 2, 1)),
        "lng_t": np.ascontiguousarray(ln_g.reshape(DEPTH, 2, KC, 128).transpose(0, 1, 3, 2)),
        "lnb_t": np.ascontiguousarray(ln_b.reshape(DEPTH, 2, KC, 128).transpose(0, 1, 3, 2)),
        "sinks_rep": np.ascontiguousarray(np.broadcast_to(attn_sinks[:, None, :], (DEPTH, 128, H_B))),
        "lam_rep": np.ascontiguousarray(np.broadcast_to(diff_lambda.reshape(DEPTH, 1, 256), (DEPTH, 128, 256))),
        "subg_t": np.ascontiguousarray(diff_subln_g.T),
    }
    maps = []
    nblk = SL // 512
    for r in range(NCORES):
        b, par = r // 2, r % 2
        m = dict(shared)
        m.update(consts[par])
        xl = x[b].reshape(nblk, 2, 512, D)[:, par].reshape(SL, D)
        m["xT"] = np.ascontiguousarray(xl.T).reshape(KC, 128, SL)
        m["cTs"] = np.ascontiguousarray(c[:, r * 512:(r + 1) * 512].reshape(4, 4, 128).transpose(2, 1, 0))
        bs = np.zeros((4, 2), np.float32)
        bs[b, :] = 1.0
        m["bsel"] = bs
        m["w_ada_s"] = np.ascontiguousarray(w_ada[:, r * 512:(r + 1) * 512, :])
        m["w_in_s"] = np.ascontiguousarray(w_in[:, r * 512:(r + 1) * 512, :])
        m["w_o_s"] = np.ascontiguousarray(w_o[:, r * 512:(r + 1) * 512, :])
        m["w_gate_s"] = np.ascontiguousarray(w_gate[:, r * 512:(r + 1) * 512, :])
        m["w_up_s"] = np.ascontiguousarray(w_up[:, r * 512:(r + 1) * 512, :])
        m["w_down_s"] = np.ascontiguousarray(w_down[:, r * 1376:(r + 1) * 1376, :])
        maps.append(m)
    return maps


def assemble_output(results, B, S):
    SL = S // 2
    nblk = SL // 512
    out = np.empty((B, S, D), np.float32)
    ov = out.reshape(B, nblk, 2, 512, D)
    for r in range(NCORES):
        b, par = r // 2, r % 2
        ov[b, :, par] = results[r]["outT"].reshape(D, SL).T.reshape(nblk, 512, D)
    return out


def kernel(x, c, rel_bias, w_ada, b_ada, w_in, w_o, attn_sinks, diff_lambda, diff_subln_g,
           ln_g, ln_b, w_gate, w_up, w_down):
    args = [np.asarray(a, dtype=np.float32) for a in (x, c, rel_bias, w_ada, b_ada, w_in, w_o, attn_sinks, diff_lambda,
                                                      diff_subln_g, ln_g, ln_b, w_gate, w_up, w_down)]
    x = args[0]
    B, S, _ = x.shape
    SL = S // 2
    maps = host_inputs(SL, *args)
    nc = build_program(SL)
    res = run_bass_kernel_spmd(nc, maps, core_ids=list(range(NCORES)))
    return assemble_output(res.results, B, S)
```
